# Optimizing a Trainium2 kernel written in Bass

```python
import jax, jax.numpy as jnp
from jax import lax
import numpy as np

D_MODEL = 1024
BATCH = 16
SEQ = 256
DEPTH = 2
DEC_BATCH = 4
DEC_SEQ = 1024
PAST_LEN = 256

GRID_W = 64
H_A = 8
DK_A = 128
DV_A = 128
H_B = 4
DK_B = 256
DV_B = 512
D_FF = 2816
CONV_W = 3
CHUNK = 16
ROPE_BASE = 10000.0
EPS = 1e-6
RET_DECAY_OFFSET_BWD = 0.5
SPLIT_SIZES = (H_A * DK_A, H_A * DK_A, H_A * DK_A, H_A * DV_A, H_A * DV_A,
               H_B * DK_B, H_B * DK_B, H_B * DV_B, H_B * DV_B, D_MODEL, D_MODEL)
IN_WIDTH = sum(SPLIT_SIZES)

kernel_name = "hgrn2_retention_diffusion_step"


def rmsnorm(x, g):
    xf = x.astype(jnp.float32)
    y = xf * lax.rsqrt(jnp.mean(xf * xf, axis=-1, keepdims=True) + EPS)
    return (y * g.astype(jnp.float32)).astype(x.dtype)


def to_heads(t, n_heads):
    b, s, _ = t.shape
    return t.reshape(b, s, n_heads, -1).transpose(0, 2, 1, 3)


def chunk_scan(q, k, v, log_a, s0):
    b_, h_, t_, dk = q.shape
    dv = v.shape[-1]
    n = t_ // CHUNK
    r = lambda t: t.reshape(b_, h_, n, CHUNK, t.shape[-1])
    q, k, v, log_a = r(q), r(k), r(v), r(log_a)
    cum = jnp.cumsum(log_a, axis=-2)
    cum_end = cum[..., -1:, :]
    diff = cum[..., :, None, :] - cum[..., None, :, :]
    lower = jnp.tril(jnp.ones((CHUNK, CHUNK), dtype=bool))[:, :, None]
    dec = jnp.where(lower, jnp.exp(jnp.minimum(diff, 0.0)), 0.0)
    scores = jnp.einsum('bhntd,bhnsd,bhntsd->bhnts', q, k, dec)
    o_intra = jnp.einsum('bhnts,bhnsv->bhntv', scores, v)
    q_dec = q * jnp.exp(cum)
    k_dec = k * jnp.exp(cum_end - cum)
    a_end = jnp.exp(cum_end[..., 0, :])

    def step(S, xs):
        qd, kd, vv, ae = xs
        o = jnp.einsum('bhtd,bhdv->bhtv', qd, S)
        S = ae[..., None] * S + jnp.einsum('bhtd,bhtv->bhdv', kd, vv)
        return S, o

    mv = lambda t: jnp.moveaxis(t, 2, 0)
    s_fin, o_inter = lax.scan(step, s0, (mv(q_dec), mv(k_dec), mv(v), mv(a_end)))
    o = o_intra + jnp.moveaxis(o_inter, 0, 2)
    return o.reshape(b_, h_, t_, dv), s_fin


def bidir_scan(q, k_f, k_b, v, la_f, la_b, s0_f, s0_b):
    flip = lambda t: jnp.flip(t, axis=2)
    o_f, s_f = chunk_scan(q, k_f, v, la_f, s0_f)
    o_b, s_b = chunk_scan(flip(q), flip(k_b), flip(v), flip(la_b), s0_b)
    return o_f + flip(o_b), s_f, s_b


def axial_rope(x):
    t_ = x.shape[2]
    rows = t_ // GRID_W
    r_idx = jnp.repeat(jnp.arange(rows), GRID_W).astype(jnp.float32)
    c_idx = jnp.tile(jnp.arange(GRID_W), rows).astype(jnp.float32)
    half = x.shape[-1] // 2
    quarter = half // 2
    inv = 1.0 / (ROPE_BASE ** (jnp.arange(quarter, dtype=jnp.float32) / quarter))

    def rot(xh, pos):
        ang = pos[:, None] * inv[None, :]
        cos, sin = jnp.cos(ang), jnp.sin(ang)
        x1, x2 = xh[..., :quarter], xh[..., quarter:]
        return jnp.concatenate([x1 * cos - x2 * sin, x1 * sin + x2 * cos], axis=-1)

    xf = x.astype(jnp.float32)
    return jnp.concatenate([rot(xf[..., :half], r_idx), rot(xf[..., half:], c_idx)], axis=-1)


def retention_log_decay(offset):
    return jnp.log1p(-jnp.exp2(-(5.0 + offset) - jnp.arange(H_B, dtype=jnp.float32)))


def gated_head_norm(o, g, dtype):
    o = o * lax.rsqrt(jnp.mean(o * o, axis=-1, keepdims=True) + EPS)
    b_, h_, t_, dv = o.shape
    o = o.transpose(0, 2, 1, 3).reshape(b_, t_, h_ * dv)
    return (o * jax.nn.silu(g.astype(jnp.float32))).astype(dtype)


def token_mixer(h, w_in, lb, p_a, p_b, w_out, s_hgrn, s_ret, latent):
    f32 = jnp.float32
    b_, t_, _ = h.shape
    proj = jnp.einsum('btd,de->bte', h, w_in)
    points = np.cumsum(SPLIT_SIZES)[:-1].tolist()
    q_a, z_f, z_b, i_a, g_a, q_b, k_b, v_b, g_b, gate_a, gate_b = jnp.split(proj, points, axis=-1)

    def forget(z, lb_dir):
        z = to_heads(z, H_A).astype(f32)
        lbd = lb_dir.reshape(H_A, 1, DK_A).astype(f32)
        log_f = jnp.logaddexp(jnp.log(lbd), jnp.log1p(-lbd) + jax.nn.log_sigmoid(z))
        key = (1.0 - lbd) * jax.nn.sigmoid(-z)
        return log_f, key

    la_f, kh_f = forget(z_f, lb[0])
    la_b, kh_b = forget(z_b, lb[1])
    qa = to_heads(q_a, H_A).astype(f32)
    va = to_heads(i_a, H_A).astype(f32)
    o_a, sa_f, sa_b = bidir_scan(qa, kh_f, kh_b, va, la_f, la_b,
                                 s_hgrn[:, 0].astype(f32), s_hgrn[:, 1].astype(f32))
    y_a = jnp.einsum('bte,ed->btd', gated_head_norm(o_a, g_a, h.dtype), p_a)

    qb = to_heads(q_b, H_B).astype(f32)
    kb = to_heads(k_b, H_B).astype(f32) * (DK_B ** -0.5)
    if latent:
        qb, kb = axial_rope(qb), axial_rope(kb)
    vb = to_heads(v_b, H_B).astype(f32)
    shape_b = (b_, H_B, t_, DK_B)
    lg_f = jnp.broadcast_to(retention_log_decay(0.0)[None, :, None, None], shape_b)
    lg_b = jnp.broadcast_to(retention_log_decay(RET_DECAY_OFFSET_BWD)[None, :, None, None], shape_b)
    o_b, sr_f, sr_b = bidir_scan(qb, kb, kb, vb, lg_f, lg_b,
                                 s_ret[:, 0].astype(f32), s_ret[:, 1].astype(f32))
    y_b = jnp.einsum('bte,ed->btd', gated_head_norm(o_b, g_b, h.dtype), p_b)

    merged = jax.nn.sigmoid(gate_a) * y_a + jax.nn.sigmoid(gate_b) * y_b
    y = jnp.einsum('btd,de->bte', merged, w_out)
    return y, jnp.stack([sa_f, sa_b], axis=1), jnp.stack([sr_f, sr_b], axis=1)


def conv_ffn(h, w_up, w_conv, b_conv, w_down):
    u = jnp.einsum('btd,df->btf', h, w_up)
    up = jnp.pad(u, ((0, 0), (1, 1), (0, 0)))
    u = up[:, :-2] * w_conv[0] + up[:, 1:-1] * w_conv[1] + up[:, 2:] * w_conv[2] + b_conv
    a, g = jnp.split(u, 2, axis=-1)
    return jnp.einsum('btf,fd->btd', jax.nn.silu(g) * a, w_down)


def modulation(cvec, w_mod, b_mod):
    m = jnp.einsum('bd,de->be', jax.nn.silu(cvec), w_mod) + b_mod
    return jnp.split(m[:, None, :], 6, axis=-1)


def trunk_layer(x, mods, n1, n2, w_in, lb, p_a, p_b, w_out, w_up, w_conv, b_conv, w_down, s_hgrn, s_ret, latent):
    sh1, sc1, gt1, sh2, sc2, gt2 = mods
    h = rmsnorm(x, n1) * (1.0 + sc1) + sh1
    y, st_hgrn, st_ret = token_mixer(h, w_in, lb, p_a, p_b, w_out, s_hgrn, s_ret, latent)
    x = x + gt1 * y
    h = rmsnorm(x, n2) * (1.0 + sc2) + sh2
    x = x + gt2 * conv_ffn(h, w_up, w_conv, b_conv, w_down)
    return x, st_hgrn, st_ret


def setup_inputs(seed: int = 0) -> dict:
    key = jax.random.key(seed)
    ks = jax.random.split(key, 20)
    nrm = lambda k, shape, s: jax.random.normal(k, shape, jnp.float32) * s
    return {
        "x_prompt": nrm(ks[0], (BATCH, SEQ, D_MODEL), 1.0),
        "x_sample": nrm(ks[1], (DEC_BATCH, DEC_SEQ, D_MODEL), 1.0),
        "state_hgrn": nrm(ks[2], (DEC_BATCH, DEPTH, 2, H_A, DK_A, DV_A), 0.5),
        "state_ret": nrm(ks[3], (DEC_BATCH, DEPTH, 2, H_B, DK_B, DV_B), 0.5),
        "c": nrm(ks[4], (DEC_BATCH, D_MODEL), 1.0),
        "c_ctx": nrm(ks[5], (D_MODEL,), 1.0),
        "norm1": 1.0 + nrm(ks[6], (DEPTH, D_MODEL), 0.02),
        "norm2": 1.0 + nrm(ks[7], (DEPTH, D_MODEL), 0.02),
        "final_norm": 1.0 + nrm(ks[8], (D_MODEL,), 0.02),
        "w_mod": nrm(ks[9], (DEPTH, D_MODEL, 6 * D_MODEL), 0.5 * D_MODEL ** -0.5),
        "b_mod": nrm(ks[10], (DEPTH, 6 * D_MODEL), 0.01),
        "w_in": nrm(ks[11], (DEPTH, D_MODEL, IN_WIDTH), D_MODEL ** -0.5),
        "hgrn_lb_raw": nrm(ks[12], (DEPTH, 2, H_A * DK_A), 1.0),
        "p_a": nrm(ks[13], (DEPTH, H_A * DV_A, D_MODEL), (H_A * DV_A) ** -0.5),
        "p_b": nrm(ks[14], (DEPTH, H_B * DV_B, D_MODEL), (H_B * DV_B) ** -0.5),
        "w_out": nrm(ks[15], (DEPTH, D_MODEL, D_MODEL), D_MODEL ** -0.5),
        "w_up": nrm(ks[16], (DEPTH, D_MODEL, 2 * D_FF), D_MODEL ** -0.5),
        "w_conv": nrm(ks[17], (DEPTH, CONV_W, 2 * D_FF), CONV_W ** -0.5),
        "b_conv": nrm(ks[18], (DEPTH, 2 * D_FF), 0.01),
        "w_down": nrm(ks[19], (DEPTH, D_FF, D_MODEL), D_FF ** -0.5),
    }


def reference(x_prompt, x_sample, state_hgrn, state_ret, c, c_ctx, norm1, norm2, final_norm,
              w_mod, b_mod, w_in, hgrn_lb_raw, p_a, p_b, w_out, w_up, w_conv, b_conv, w_down):
    sm = jax.nn.softmax(hgrn_lb_raw.astype(jnp.float32), axis=0)
    cum = jnp.cumsum(sm, axis=0)
    lower_bounds = cum - cum[0:1]

    x = x_prompt
    b_ctx = x_prompt.shape[0]
    zero_hgrn = jnp.zeros((b_ctx, 2, H_A, DK_A, DV_A), jnp.float32)
    zero_ret = jnp.zeros((b_ctx, 2, H_B, DK_B, DV_B), jnp.float32)
    hgrn_states, ret_states = [], []
    for l in range(DEPTH):
        mods = modulation(c_ctx[None, :], w_mod[l], b_mod[l])
        x, st_h, st_r = trunk_layer(x, mods, norm1[l], norm2[l], w_in[l], lower_bounds[l], p_a[l], p_b[l],
                                    w_out[l], w_up[l], w_conv[l], b_conv[l], w_down[l],
                                    zero_hgrn, zero_ret, False)
        hgrn_states.append(st_h)
        ret_states.append(st_r)
    y_prompt = rmsnorm(x, final_norm)
    new_state_hgrn = jnp.stack(hgrn_states, axis=1).astype(x_prompt.dtype)
    new_state_ret = jnp.stack(ret_states, axis=1).astype(x_prompt.dtype)

    x = x_sample
    for l in range(DEPTH):
        mods = modulation(c, w_mod[l], b_mod[l])
        x, _, _ = trunk_layer(x, mods, norm1[l], norm2[l], w_in[l], lower_bounds[l], p_a[l], p_b[l],
                              w_out[l], w_up[l], w_conv[l], b_conv[l], w_down[l],
                              state_hgrn[:, l], state_ret[:, l], True)
    y_sample = rmsnorm(x, final_norm)
    return (y_prompt, y_sample, new_state_hgrn, new_state_ret)
```

```python
import numpy as np
import concourse.bass as bass
import concourse.mybir as mybir
from concourse.bass_utils import run_bass_kernel_spmd
from contextlib import ExitStack

F32 = mybir.dt.float32
BF16 = mybir.dt.bfloat16
AF = mybir.ActivationFunctionType
ALU = mybir.AluOpType
DSZ = {F32: 4, BF16: 2}
CELL = 128
ENGS = ("pe", "act", "dve", "pool", "sp")


class V:
    __slots__ = ("ap", "space", "lo", "hi")

    def __init__(s, ap, space, lo, hi):
        s.ap, s.space, s.lo, s.hi = ap, space, lo, hi

    def r(s, pat, **kw):
        return V(s.ap.rearrange(pat, **kw), s.space, s.lo, s.hi)

    def p(s, a, b):
        return V(s.ap[a:b], s.space, s.lo, s.hi)

    def idx(s, *key):
        return V(s.ap[key], s.space, s.lo, s.hi)


class T:
    def __init__(s, base_ap, space, off, n, dt):
        s.space, s.off, s.n, s.dt = space, off, n, dt
        s.ap = base_ap

    def v(s, a=0, b=None):
        b = s.n if b is None else b
        sz = DSZ[s.dt]
        return V(s.ap[:, a:b], s.space, s.off + a * sz, s.off + b * sz)


class Prog:
    def __init__(s, nc, sb_words=53184, ndma_sem=10):
        s.nc = nc
        s.es = ExitStack()
        s.sb_words = sb_words
        s.arenaF = s.es.enter_context(nc.sbuf_tensor("arenaF", [128, sb_words], F32))
        s.arenaH = s.arenaF[:].bitcast(BF16)
        s.psum = s.es.enter_context(nc.psum_tensor("psumT", [128, 4096], F32))
        s.free = [(0, sb_words * 4)]
        s.allocs = {}
        s.ops = {e: [] for e in ENGS}
        s.cells = {}
        s.waited = {e: {} for e in ENGS}
        s.sem = {e: s.es.enter_context(nc.semaphore("sem_" + e)) for e in ENGS}
        s.dq = {}
        for q in ("sp", "act", "pool"):
            s.dq[q] = dict(sems=[s.es.enter_context(nc.semaphore(f"dma_{q}_{i}")) for i in range(ndma_sem)],
                           cnt=[0] * ndma_sem, nxt=0)
        s.out_events = []

    def alloc(s, name, n, dt=F32):
        nbytes = ((n * DSZ[dt] + CELL - 1) // CELL) * CELL
        for i, (o, sz) in enumerate(s.free):
            if sz >= nbytes:
                if sz == nbytes:
                    s.free.pop(i)
                else:
                    s.free[i] = (o + nbytes, sz - nbytes)
                s.allocs[name] = (o, nbytes)
                if dt == F32:
                    ap = s.arenaF[:, o // 4:o // 4 + n]
                else:
                    ap = s.arenaH[:, o // 2:o // 2 + n]
                return T(ap, "S", o, n, dt)
        raise RuntimeError(f"SBUF arena full allocating {name} {nbytes} free={s.free}")

    def release(s, name):
        o, nbytes = s.allocs.pop(name)
        s.free.append((o, nbytes))
        s.free.sort()
        m = []
        for o, sz in s.free:
            if m and m[-1][0] + m[-1][1] == o:
                m[-1] = (m[-1][0], m[-1][1] + sz)
            else:
                m.append((o, sz))
        s.free = m

    def bank(s, k, a=0, b=512):
        return V(s.psum[:, k * 512 + a:k * 512 + b], "P", (k * 512 + a) * 4, (k * 512 + b) * 4)

    def _cells(s, v):
        c = 2048 if v.space == "P" else CELL
        return [(v.space, i) for i in range(v.lo // c, (v.hi - 1) // c + 1)]

    def _deps(s, eng, reads, writes):
        need = {}

        def add(ev):
            if ev is None:
                return
            k = ev[0]
            if k not in need or need[k] < ev[1]:
                need[k] = ev[1]

        for v in reads:
            if v.space == "D":
                continue
            for c in s._cells(v):
                st = s.cells.get(c)
                if st:
                    add(st["w"])
                    if v.space == "P":
                        for ev in st["r"].values():
                            if ev[0] != eng:
                                add(ev)
        for v in writes:
            if v.space == "D":
                continue
            for c in s._cells(v):
                st = s.cells.get(c)
                if st:
                    add(st["w"])
                    for ev in st["r"].values():
                        add(ev)
        out = []
        wd = s.waited[eng]
        for k, val in need.items():
            if k == eng and eng in ("pe", "sp"):
                continue
            if wd.get(k, 0) >= val:
                continue
            wd[k] = val
            out.append((k, val))
        return out

    def _mark(s, ev, reads, writes):
        for v in reads:
            if v.space == "D":
                continue
            for c in s._cells(v):
                st = s.cells.setdefault(c, {"w": None, "r": {}})
                st["r"][ev[0]] = ev
        for v in writes:
            if v.space == "D":
                continue
            for c in s._cells(v):
                s.cells[c] = {"w": ev, "r": {}}

    def op(s, eng, fn, reads, writes):
        deps = s._deps(eng, reads, writes)
        seq = len(s.ops[eng]) + 1
        s.ops[eng].append(dict(fn=fn, deps=deps, dma=None))
        s._mark((eng, seq), reads, writes)

    def dma(s, out, in_, q="sp"):
        dq = s.dq[q]
        i = dq["nxt"]
        dq["nxt"] = (i + 1) % len(dq["sems"])
        key = ("dma", q, i)
        deps = s._deps(q, [in_], [out])
        prev = dq["cnt"][i]
        if prev > 0 and s.waited[q].get(key, 0) < prev:
            s.waited[q][key] = prev
            deps.append((key, prev))
        dq["cnt"][i] = prev + 16
        ev = (key, prev + 16)
        o_ap, i_ap = out.ap, in_.ap
        s.ops[q].append(dict(fn=lambda e: e.dma_start(out=o_ap, in_=i_ap), deps=deps, dma=(q, i)))
        s._mark(ev, [in_], [out])
        if out.space == "D":
            s.out_events.append(ev)

    def mm(s, out, lhsT, rhs, start=True, stop=True):
        o, l, r = out.ap, lhsT.ap, rhs.ap
        s.op("pe", lambda e: e.matmul(o, l, r, start=start, stop=stop, skip_group_check=True), [lhsT, rhs], [out])

    def tr(s, out, in_, ident):
        o, i, d = out.ap, in_.ap, ident.ap
        s.op("pe", lambda e: e.transpose(o, i, d), [in_, ident], [out])

    def act(s, out, in_, func, scale=1.0, bias=0.0, accum=None):
        o, i = out.ap, in_.ap
        sc = scale.ap if isinstance(scale, V) else scale
        bi = bias.ap if isinstance(bias, V) else bias
        rd = [in_] + [x for x in (scale, bias) if isinstance(x, V)]
        wr = [out]
        kw = {}
        if accum is not None:
            kw["accum_out"] = accum.ap
            wr.append(accum)
        s.op("act", lambda e: e.activation(o, i, func, bias=bi, scale=sc, **kw), rd, wr)

    def tt(s, out, a, b, op, eng="dve"):
        o, x, y = out.ap, a.ap, b.ap
        s.op(eng, lambda e: e.tensor_tensor(o, x, y, op), [a, b], [out])

    def ts(s, out, a, s1, s2, op0, op1=None, eng="dve"):
        o, x = out.ap, a.ap
        c1 = s1.ap if isinstance(s1, V) else s1
        c2 = s2.ap if isinstance(s2, V) else s2
        rd = [a] + [x_ for x_ in (s1, s2) if isinstance(x_, V)]
        if op1 is None:
            s.op(eng, lambda e: e.tensor_scalar(o, x, c1, None, op0), rd, [out])
        else:
            s.op(eng, lambda e: e.tensor_scalar(o, x, c1, c2, op0, op1), rd, [out])

    def stt(s, out, a, sc, b, op0, op1, eng="dve"):
        o, x, y = out.ap, a.ap, b.ap
        c = sc.ap if isinstance(sc, V) else sc
        rd = [a, b] + ([sc] if isinstance(sc, V) else [])
        s.op(eng, lambda e: e.scalar_tensor_tensor(o, x, c, y, op0, op1), rd, [out])

    def scan(s, out, d0, d1, init, op0, op1):
        o, x, y = out.ap, d0.ap, d1.ap
        s.op("dve", lambda e: e.tensor_tensor_scan(o, x, y, init, op0, op1), [d0, d1], [out])

    def copy(s, out, in_, eng="dve"):
        o, i = out.ap, in_.ap
        if eng == "act":
            s.op("act", lambda e: e.copy(o, i), [in_], [out])
        else:
            s.op(eng, lambda e: e.tensor_copy(o, i), [in_], [out])

    def memset(s, out, val, eng="dve"):
        o = out.ap
        s.op(eng, lambda e: e.memset(o, val), [], [out])

    def finish(s):
        nc = s.nc
        fin = {}
        for k, val in s.out_events:
            fin[k] = max(fin.get(k, 0), val)
        sig = {e: set() for e in ENGS}
        for e in ENGS:
            for o in s.ops[e]:
                for k, val in o["deps"]:
                    if isinstance(k, str):
                        sig[k].add(val)
        rank = {}
        for e in ENGS:
            rank[e] = {seq: i + 1 for i, seq in enumerate(sorted(sig[e]))}
        s.nsig = {e: len(rank[e]) for e in ENGS}
        engmap = {"pe": "tensor", "act": "scalar", "dve": "vector", "pool": "gpsimd", "sp": "sync"}

        def run(ename):
            def body(eng):
                for seq, o in enumerate(s.ops[ename], start=1):
                    for k, val in o["deps"]:
                        if isinstance(k, str):
                            eng.wait_ge(s.sem[k], rank[k][val])
                        else:
                            eng.wait_ge(s.dq[k[1]]["sems"][k[2]], val)
                    ins = o["fn"](eng)
                    if o["dma"] is not None:
                        q, i = o["dma"]
                        ins.then_inc(s.dq[q]["sems"][i], 16)
                    elif seq in rank[ename]:
                        ins.then_inc(s.sem[ename], 1)
                if ename == "sp":
                    for k, val in fin.items():
                        eng.wait_ge(s.dq[k[1]]["sems"][k[2]], val)
            return body

        with nc.Block() as block:
            for e in ENGS:
                getattr(block, engmap[e])(run(e))
        s.es.close()

import math
import ml_dtypes

NT = 1024
DM = 1024
IDN = AF.Identity
LGF = [math.log1p(-2.0 ** (-(5.0 + h))) for h in range(4)]
LGB = [math.log1p(-2.0 ** (-(5.5 + h))) for h in range(4)]
EPS = 1e-6


def Dv(ap):
    return V(ap, "D", 0, 0)


def build(nc, stage=99, dbg=None):
    dr = {}

    def din(name, shape, dt=F32):
        dr[name] = nc.dram_tensor(name, list(shape), dt, kind="ExternalInput").ap()
        return dr[name]

    xT_d = din("xT", [DM, NT])
    cv_d = din("cv", [128, 8])
    n1_d = din("n1", [128, 16]); n2_d = din("n2", [128, 16]); nf_d = din("nf", [128, 8])
    bmod_d = din("bmod", [128, 96])
    wmod_d = din("w_mod", [2, DM, 6144])
    lbraw_d = din("lbraw", [128, 32])
    wconv_d = din("wconv", [128, 2 * 3 * 44]); bconv_d = din("bconv", [128, 88])
    win_d = din("w_in", [2, DM, 13312]); pa_d = din("p_a", [2, 1024, 1024]); pb_d = din("p_b", [2, 2048, 1024])
    wout_d = din("w_out", [2, 1024, 1024]); wup_d = din("w_up", [2, 1024, 5632]); wdn_d = din("w_down", [2, 2816, 1024])
    s0h_d = din("s0h", [2, 2, 8, 128, 128]); s0r_d = din("s0r", [2, 2, 4, 256, 512])
    rope_d = din("rope", [128, 240])
    rmask_d = din("rmask", [4, 128, 1920], BF16)
    cst_d = din("cst", [128, 2560])
    yT_d = nc.dram_tensor("yT", [DM, NT], F32, kind="ExternalOutput").ap()
    sth_d = nc.dram_tensor("sth", [2, 2, 4, 8, 128, 128], F32, kind="ExternalOutput").ap()
    str_d = nc.dram_tensor("str", [2, 2, 4, 4, 256, 512], F32, kind="ExternalOutput").ap()
    if dbg:
        dbg_d = nc.dram_tensor("dbg", [dbg, 128, NT], F32, kind="ExternalOutput").ap()
    dbgi = [0]

    P = Prog(nc)
    mm, act, tt, ts, stt = P.mm, P.act, P.tt, P.ts, P.stt

    def dump(v, n=NT):
        if dbg and dbgi[0] < dbg:
            t = P.alloc("dbgt", n)
            P.copy(t.v(), v)
            P.dma(Dv(dbg_d[dbgi[0]][:, 0:n]), t.v(), q="act")
            P.release("dbgt")
            dbgi[0] += 1

    cst = P.alloc("cst", 2560)
    P.dma(cst.v(), Dv(cst_d))
    o = 0
    tpos1 = cst.v(o, o + 1024); o += 1024
    rst = cst.v(o, o + 1024); o += 1024
    identf = cst.v(o, o + 128); o += 128
    bmf = cst.v(o, o + 128); o += 128
    bmb = cst.v(o, o + 128); o += 128
    rowmask = cst.v(o, o + 4); o += 4
    kmask = cst.v(o, o + 64); o += 64
    keep = cst.v(o, o + 1); o += 1
    decf = cst.v(o, o + 8); o += 8
    decb = cst.v(o, o + 8); o += 8
    ident = P.alloc("ident", 128, BF16); P.copy(ident.v(), identf)
    onesD = P.alloc("onesD", 128, BF16); P.memset(onesD.v(), 1.0 / 1024)
    onesA = P.alloc("onesA", 128, BF16); P.memset(onesA.v(), 1.0 / 128)
    onesB = P.alloc("onesB", 128, BF16); P.memset(onesB.v(), 1.0 / 512)
    ropet = P.alloc("ropet", 240)
    P.dma(ropet.v(), Dv(rope_d))

    def ropev(dt, which, c):
        if dt == 0:
            o_ = (160 if which == 2 else which * 16) + 8 * c
            return V(ropet.ap[:, o_:o_ + 8].unsqueeze(2).broadcast_to([128, 8, 64]), "S", ropet.off, ropet.off + 960)
        o_ = 176 if which == 2 else 32 + which * 64
        return V(ropet.ap[:, o_:o_ + 64].unsqueeze(1).broadcast_to([128, 8, 64]), "S", ropet.off, ropet.off + 960)
    rmask = [P.alloc(f"rmask{h}", 1920, BF16) for h in range(4)]
    for h in range(4):
        P.dma(rmask[h].v(), Dv(rmask_d[h]))
    prm = P.alloc("prm", 8 + 16 + 16 + 8 + 96 + 32 + 264 + 88)
    o = 0
    cv = prm.v(o, o + 8); P.dma(cv, Dv(cv_d)); o += 8
    n1 = prm.v(o, o + 16); P.dma(n1, Dv(n1_d)); o += 16
    n2 = prm.v(o, o + 16); P.dma(n2, Dv(n2_d)); o += 16
    nf = prm.v(o, o + 8); P.dma(nf, Dv(nf_d)); o += 8
    bmod = prm.v(o, o + 96); P.dma(bmod, Dv(bmod_d)); o += 96
    lbraw = prm.v(o, o + 32); P.dma(lbraw, Dv(lbraw_d)); o += 32
    wconv = prm.v(o, o + 264); P.dma(wconv, Dv(wconv_d)); o += 264
    bconv = prm.v(o, o + 88); P.dma(bconv, Dv(bconv_d)); o += 88
    prm_off = prm.off

    def pv(base_v, a, b):
        return V(base_v.ap[:, a:b], "S", base_v.lo + a * 4, base_v.lo + b * 4)

    xT = [P.alloc(f"xT{k}", NT) for k in range(8)]
    for k in range(8):
        P.dma(xT[k].v(), Dv(xT_d[k * 128:(k + 1) * 128, :]))
    hT = [P.alloc(f"hT{k}", NT, BF16) for k in range(8)]

    sm = P.alloc("sm", 8 + 96 + 96 + 16 + 16 + 16 + 16 + 264)
    o = 0
    cs = sm.v(o, o + 8); o += 8
    mods = sm.v(o, o + 96); o += 96
    g12 = sm.v(o, o + 32); o += 32
    lb = sm.v(o, o + 32); o += 32
    oml = sm.v(o, o + 32); o += 32
    noml = sm.v(o, o + 32); o += 32
    wck = sm.v(o, o + 176); o += 176
    act(cs, cv, AF.Silu)
    P.memset(pv(lb, 0, 16), 0.0)
    tt(pv(lb, 16, 32), pv(lbraw, 16, 32), pv(lbraw, 0, 16), ALU.subtract)
    act(pv(lb, 16, 32), pv(lb, 16, 32), AF.Sigmoid)
    ts(oml, lb, -1.0, 1.0, ALU.mult, ALU.add)
    ts(noml, oml, -1.0, None, ALU.mult)
    for l in range(2):
        for j, tap in enumerate((0, 2)):
            ts(pv(wck, (l * 2 + j) * 44, (l * 2 + j + 1) * 44), pv(wconv, (l * 3 + tap) * 44, (l * 3 + tap + 1) * 44),
               keep, None, ALU.mult)

    NST, NBF, LA = 3, 10, 4
    wst = [P.alloc(f"wst{i}", 1024) for i in range(NST)]
    wbf = [P.alloc(f"wbf{i}", 1024, BF16) for i in range(NBF)]

    def wcols(w_l, c0, k0=0, nk=8):
        return w_l[k0 * 128:(k0 + nk) * 128, c0:c0 + 128].rearrange("(k p) c -> p k c", p=128)

    WQ = []
    for l_ in range(2):
        wl_ = win_d[l_]
        for h_ in range(8):
            for cb in (0, 1024, 2048, 3072, 4096):
                WQ.append((wcols(wl_, cb + h_ * 128), 8))
        for h_ in range(4):
            for cb in (5120, 6144):
                for dt_ in range(2):
                    WQ.append((wcols(wl_, cb + h_ * 256 + dt_ * 128), 8))
            for cb in (7168, 9216):
                for u_ in range(4):
                    WQ.append((wcols(wl_, cb + h_ * 512 + u_ * 128), 8))
        for m_ in range(8):
            WQ.append((wcols(pb_d[l_], m_ * 128, k0=0), 8)); WQ.append((wcols(pb_d[l_], m_ * 128, k0=8), 8))
            WQ.append((wcols(pa_d[l_], m_ * 128), 8))
            WQ.append((wcols(wl_, 12288 + m_ * 128), 8)); WQ.append((wcols(wl_, 11264 + m_ * 128), 8))
        for m_ in range(8):
            WQ.append((wcols(wout_d[l_], m_ * 128), 8))
        for jj_ in range(22):
            WQ.append((wcols(wup_d[l_], jj_ * 128), 8)); WQ.append((wcols(wup_d[l_], (22 + jj_) * 128), 8))
        for m_ in range(8):
            WQ.append((wcols(wdn_d[l_], m_ * 128, k0=0, nk=8), 8)); WQ.append((wcols(wdn_d[l_], m_ * 128, k0=8, nk=8), 8))
            WQ.append((wcols(wdn_d[l_], m_ * 128, k0=16, nk=6), 6))
    wq = dict(issued=0, used=0)
    CE = []
    for l_ in range(2):
        CE += ["pool"] * 40
        CE += [("pool", "act")[i_ % 2] for i_ in range(48)]
        CE += [("pool", "act", "act")[i_ % 3] for i_ in range(40 + 8 + 44 + 24)]
    assert len(CE) == len(WQ)

    def _issue():
        i = wq["issued"]
        dap, nk = WQ[i]
        st = wst[i % NST]; bf = wbf[i % NBF]
        P.dma(st.v(0, nk * 128).r("p (k c) -> p k c", c=128), Dv(dap))
        P.copy(bf.v(0, nk * 128), st.v(0, nk * 128), CE[i])
        wq["issued"] = i + 1

    def wunit(dap, nk=8):
        i = wq["used"]
        assert str(WQ[i][0]) == str(dap) and WQ[i][1] == nk, (i, str(WQ[i][0]), str(dap))
        while wq["issued"] < min(len(WQ), i + 1 + LA):
            _issue()
        wq["used"] = i + 1
        return wbf[i % NBF]

    bk = [0]
    ALL8 = (0, 1, 2, 3, 4, 5, 6, 7)

    def nbank(grp=(0, 1, 2, 3)):
        b = grp[bk[0] % len(grp)]; bk[0] += 1
        return b

    def bankh(k, a, b):
        return V(P.psum[:, k * 512:(k + 1) * 512].bitcast(BF16)[:, a:b], "P", k * 2048 + a * 2, k * 2048 + b * 2)

    F32R = mybir.dt.float32r
    one11 = P.alloc("one11", 1); P.memset(one11.v(), 1.0)

    def r32(v):
        return V(v.ap.bitcast(F32R), v.space, v.lo, v.hi)

    def do_mods(l, part=None):
        wms = [P.alloc("wm0", 8 * 512), P.alloc("wm1", 8 * 512)]
        mrow = P.alloc("mrow", 1536)
        mb = 4
        pcs = range(4) if part is None else (range(0, 2) if part == 0 else range(2, 4))
        for pc in pcs:
            for c3 in range(3):
                j12 = pc * 3 + c3
                wm = wms[j12 % 2]
                P.dma(wm.v().r("p (k c) -> p k c", c=512),
                      Dv(wmod_d[l][:, j12 * 512:(j12 + 1) * 512].rearrange("(k p) c -> p k c", p=128)))
                b = nbank((0, 1, 2, 3))
                for k in range(8):
                    mm(P.bank(b).p(0, 1), pv(cs, k, k + 1), wm.v(k * 512, k * 512 + 512), start=(k == 0), stop=(k == 7))
                P.copy(mrow.v(c3 * 512, c3 * 512 + 512).p(0, 1), P.bank(b).p(0, 1), "act")
            for jj in range(12):
                j = pc * 12 + jj
                mm(P.bank(mb, j, j + 1), mrow.v(jj * 128, jj * 128 + 128).p(0, 1), one11.v().p(0, 1))
        c0_, c1_ = (0, 48) if part is None else ((0, 24) if part == 0 else (24, 48))
        tt(pv(mods, l * 48 + c0_, l * 48 + c1_), P.bank(mb, c0_, c1_), pv(bmod, l * 48 + c0_, l * 48 + c1_), ALU.add)
        for i, (nrm, sc_off) in enumerate(((n1, 8), (n2, 32))):
            if part is not None and i != part:
                continue
            gv = pv(g12, (l * 2 + i) * 8, (l * 2 + i) * 8 + 8)
            ts(gv, pv(mods, l * 48 + sc_off, l * 48 + sc_off + 8), 1.0, None, ALU.add)
            tt(gv, gv, pv(nrm, l * 8, l * 8 + 8), ALU.mult)
        P.release("wm0"); P.release("wm1"); P.release("mrow")

    do_mods(0, part=0)

    def mod(l, which, k):
        return pv(mods, l * 48 + which * 8 + k, l * 48 + which * 8 + k + 1)

    def rstd_from(ssb, out_v, n=512):
        act(out_v, ssb, AF.Ln, bias=EPS)
        act(out_v, out_v, AF.Exp, scale=-0.5)

    def rmsnorm(gsel, shsel, l, dst, final=False):
        sq = P.alloc("rn_sq", 512, BF16)
        rs = P.alloc("rn_rs", 512)
        tmp = P.alloc("rn_tmp", 512)
        for c in range(2):
            b = nbank()
            for k in range(8):
                act(sq.v(), xT[k].v(c * 512, c * 512 + 512), AF.Square)
                mm(P.bank(b), onesD.v(), sq.v(), start=(k == 0), stop=(k == 7))
            rstd_from(P.bank(b), rs.v())
            for k in range(8):
                tt(tmp.v(), xT[k].v(c * 512, c * 512 + 512), rs.v(), ALU.mult)
                if final:
                    ts(dst[k].v(c * 512, c * 512 + 512), tmp.v(), pv(nf, k, k + 1), None, ALU.mult)
                else:
                    act(dst[k].v(c * 512, c * 512 + 512), tmp.v(), IDN,
                        scale=pv(g12, (l * 2 + gsel) * 8 + k, (l * 2 + gsel) * 8 + k + 1), bias=mod(l, shsel, k))
        P.release("rn_sq"); P.release("rn_rs"); P.release("rn_tmp")

    def proj_fm(wu, c, src=None, grp=(0, 1, 2, 3)):
        src = src or hT
        b = nbank(grp)
        for k in range(8):
            mm(P.bank(b), wu.v(k * 128, k * 128 + 128), src[k].v(c * 512, c * 512 + 512), start=(k == 0), stop=(k == 7))
        return P.bank(b)

    for l in range(2):
        if stage < 1:
            break
        gatedA = [P.alloc(f"gA{h}", NT, BF16) for h in range(8)]
        rmsnorm(0, 0, l, hT)
        if l == 0:
            while wq["issued"] < LA:
                _issue()
            do_mods(0, part=1)
        if l == 0 and dbg:
            dump(hT[0].v()); dump(hT[7].v())
        if stage < 2:
            break
        wl = win_d[l]
        for h in range(8):
            q32 = P.alloc("q32", NT); sgl = P.alloc("sgl", NT, BF16)
            sig = P.alloc("sig", NT); la = [P.alloc("laf", NT), P.alloc("lab", NT)]
            key = [P.alloc("keyf", NT), P.alloc("keyb", NT)]
            vtok = P.alloc("vtok", 8 * 128, BF16)
            vm = [P.alloc(f"vm{q}", 8 * 128, BF16) for q in range(4)]
            wu = wunit(wcols(wl, h * 128))
            for c in range(2):
                act(q32.v(c * 512, c * 512 + 512), proj_fm(wu, c), AF.Copy)
            sigs = [sig, P.alloc("sig2", NT)]
            for d in range(2):
                wu = wunit(wcols(wl, 1024 * (1 + d) + h * 128))
                for c in range(2):
                    sl = (c * 512, c * 512 + 512)
                    act(sigs[d].v(*sl), proj_fm(wu, c), AF.Sigmoid)
            for d in range(2):
                lbv = pv(lb, (l * 2 + d) * 8 + h, (l * 2 + d) * 8 + h + 1)
                omv = pv(oml, (l * 2 + d) * 8 + h, (l * 2 + d) * 8 + h + 1)
                nomv = pv(noml, (l * 2 + d) * 8 + h, (l * 2 + d) * 8 + h + 1)
                act(la[d].v(), sigs[d].v(), AF.Ln, scale=omv, bias=lbv)
                ts(key[d].v(), sigs[d].v(), nomv, omv, ALU.mult, ALU.add)
            P.release("sig"); P.release("sig2")
            wu = wunit(wcols(wl, 3072 + h * 128))
            vT = P.alloc("vT", NT, BF16)
            for c in range(2):
                act(vT.v(c * 512, c * 512 + 512), proj_fm(wu, c), AF.Copy)
            tbv = nbank()
            for j in range(8):
                P.tr(bankh(tbv, j * 128, j * 128 + 128), vT.v(j * 128, j * 128 + 128), ident.v())
            P.copy(vtok.v(), bankh(tbv, 0, 1024), "dve")
            for q in range(4):
                ts(vm[q].v(), bankh(tbv, 0, 1024), pv(rowmask, q, q + 1), None, ALU.mult)
            P.release("vT")
            wu = wunit(wcols(wl, 4096 + h * 128))
            for c in range(2):
                act(sgl.v(c * 512, c * 512 + 512), proj_fm(wu, c), AF.Silu)
            if l == 0 and h == 0 and dbg:
                dump(q32.v()); dump(la[0].v()); dump(key[1].v())
            D = []
            T_ = [dict() for _ in range(2)]
            for d in range(2):
                t_ = T_[d]
                t_["bsc"] = P.alloc(f"bsc{d}", NT); t_["E1"] = P.alloc(f"E1{d}", NT); t_["E2"] = P.alloc(f"E2{d}", NT)
                t_["qt"] = P.alloc(f"qt{d}", NT, BF16); t_["kt"] = P.alloc(f"kt{d}", NT, BF16)
                t_["ktok"] = P.alloc(f"ktok{d}", 8 * 128, BF16); t_["PT"] = P.alloc(f"PT{d}", 8 * 128, BF16)
                t_["Atab"] = P.alloc(f"Atab{d}", 64)
                s0 = P.alloc(f"s0{d}", 128)
                P.dma(s0.v(), Dv(s0h_d[l, d, h]))
                t_["s0"] = s0
            P.scan(T_[0]["bsc"].v(), rst, la[0].v(), 0.0, ALU.mult, ALU.add)
            P.scan(V(T_[1]["bsc"].v().ap[:, ::-1], "S", T_[1]["bsc"].off, T_[1]["bsc"].off + 4096), rst,
                   V(la[1].v().ap[:, ::-1], "S", la[1].off, la[1].off + 4096), 0.0, ALU.mult, ALU.add)
            for d in range(2):
                act(T_[d]["E1"].v(), T_[d]["bsc"].v(), AF.Exp)
                act(T_[d]["E2"].v(), T_[d]["bsc"].v(), AF.Exp, scale=-1.0)
            for d in range(2):
                t_ = T_[d]
                tt(t_["qt"].v(), q32.v(), t_["E1"].v(), ALU.mult)
                tt(t_["kt"].v(), key[d].v(), t_["E2"].v(), ALU.mult)
                e1b = t_["E1"].v().r("p (n c) -> p n c", c=32)
                col = 31 if d == 0 else 0
                P.copy(t_["Atab"].v(0, 32), e1b.idx(slice(None), slice(None), col))
                tt(t_["Atab"].v(32, 64), t_["Atab"].v(0, 32), pv(kmask, d * 32, d * 32 + 32), ALU.mult)
            for d in range(2):
                t_ = T_[d]
                tbk = nbank()
                for j in range(8):
                    P.tr(bankh(tbk, j * 128, j * 128 + 128), t_["kt"].v(j * 128, j * 128 + 128), ident.v())
                P.copy(t_["ktok"].v(), bankh(tbk, 0, 1024), "dve")
                bm = bmf if d == 0 else bmb
                for jg in range(2):
                    b = nbank()
                    for jj in range(4):
                        j = jg * 4 + jj
                        mm(P.bank(b, jj * 128, jj * 128 + 128), t_["kt"].v(j * 128, j * 128 + 128), t_["qt"].v(j * 128, j * 128 + 128))
                    bmv = V(bm.ap.unsqueeze(1).broadcast_to([128, 4, 128]), "S", bm.lo, bm.hi)
                    tt(t_["PT"].v(jg * 512, jg * 512 + 512).r("p (a b) -> p a b", b=128), P.bank(b).r("p (a b) -> p a b", b=128),
                       bmv, ALU.mult)
            if l == 0 and h == 0 and dbg:
                dump(T_[0]["bsc"].v()); dump(T_[0]["E2"].v())
            for d in range(2):
                t_ = T_[d]
                for nm in (f"bsc{d}", f"E1{d}", f"E2{d}", f"kt{d}", "laf" if d == 0 else "lab", "keyf" if d == 0 else "keyb"):
                    P.release(nm)
                W = [P.alloc(f"Wst{d}", 128), P.alloc(f"Wsu{d}", 128)]; Sbf = [P.alloc(f"Sbf0{d}", 128, BF16), P.alloc(f"Sbf1{d}", 128, BF16)]
                sfin = P.alloc(f"sfin{d}", 128)
                D.append(dict(qt=t_["qt"], ktok=t_["ktok"], PT=t_["PT"], Atab=t_["Atab"], W=W, Sbf=Sbf, s0=t_["s0"], sfin=sfin, prev=None,
                              order=list(range(32)) if d == 0 else list(range(31, -1, -1))))
            P.release("q32")
            ob_ = [P.alloc("obf", NT), P.alloc("obb", NT)]
            obank = [(0, 1), (6, 7)]
            kvbank = [(2, 3), (4, 5)]
            for g in range(8):
                for d in range(2):
                    st = D[d]
                    j = st["order"][4 * g] // 4
                    for q in range(4):
                        n = st["order"][4 * g + q]
                        mm(P.bank(kvbank[d][g % 2], q * 128, q * 128 + 128), st["ktok"].v(j * 128, j * 128 + 128),
                           vm[n % 4].v(j * 128, j * 128 + 128))
                    ob = obank[d][(j // 4) % 2]
                    jj = j % 4
                    mm(P.bank(ob, jj * 128, jj * 128 + 128), vtok.v(j * 128, j * 128 + 128), st["PT"].v(j * 128, j * 128 + 128),
                       start=True, stop=False)
                for q in range(4):
                    for d in range(2):
                        st = D[d]
                        i = 4 * g + q
                        n = st["order"][i]
                        j = n // 4
                        jj = j % 4
                        ob = obank[d][(j // 4) % 2]
                        Atab = st["Atab"]; W = st["W"]; prev = st["prev"]
                        ocols = P.bank(ob, jj * 128 + (n % 4) * 32, jj * 128 + (n % 4) * 32 + 32)
                        sb = st["Sbf"][i % 2]
                        kvb = P.bank(kvbank[d][g % 2], q * 128, q * 128 + 128)
                        Wn, Wp = W[i % 2], W[(i + 1) % 2]
                        if prev is None:
                            P.copy(sb.v(), st["s0"].v(), "act")
                            tt(Wn.v(), st["s0"].v(), kvb, ALU.add)
                        else:
                            act(sb.v(), Wp.v(), IDN, scale=pv(Atab.v(), 32 + prev, 32 + prev + 1))
                            stt(Wn.v(), Wp.v(), pv(Atab.v(), 32 + prev, 32 + prev + 1), kvb, ALU.mult, ALU.add)
                        mm(ocols, sb.v(), st["qt"].v(n * 32, n * 32 + 32), start=False, stop=(q == 3))
                        st["prev"] = n
                        if i % 8 == 7:
                            seq = n // 8
                            act(st["sfin"].v(), Wn.v(), IDN, scale=pv(Atab.v(), n, n + 1))
                            P.dma(Dv(sth_d[l, d, seq, h]), st["sfin"].v(), q="sp")
                        if i % 16 == 15:
                            c0 = (j // 4) * 512
                            P.copy(ob_[d].v(c0, c0 + 512), P.bank(ob), "dve")
            obuf = ob_[0]
            tt(obuf.v(), ob_[0].v(), ob_[1].v(), ALU.add)
            for d in range(2):
                for nm in (f"qt{d}", f"ktok{d}", f"PT{d}", f"Atab{d}", f"Wst{d}", f"Wsu{d}", f"Sbf0{d}", f"Sbf1{d}", f"s0{d}", f"sfin{d}"):
                    P.release(nm)
            if l == 0 and h == 0 and dbg:
                dump(obuf.v())
            sq = P.alloc("hn_sq", NT, BF16); rs = P.alloc("hn_rs", NT); tmp = P.alloc("hn_tmp", NT)
            act(sq.v(), obuf.v(), AF.Square)
            hb = [nbank(), nbank()]
            for c in range(2):
                mm(P.bank(hb[c]), onesA.v(), sq.v(c * 512, c * 512 + 512))
            for c in range(2):
                act(rs.v(c * 512, c * 512 + 512), P.bank(hb[c]), AF.Ln, bias=EPS)
            act(rs.v(), rs.v(), AF.Exp, scale=-0.5)
            tt(tmp.v(), obuf.v(), rs.v(), ALU.mult)
            tt(gatedA[h].v(), tmp.v(), sgl.v(), ALU.mult)
            for nm in ("hn_sq", "hn_rs", "hn_tmp", "sgl", "vtok", "obf", "obb", "vm0", "vm1", "vm2", "vm3"):
                P.release(nm)
            if stage < 3 and h == 0:
                break
        if l == 0 and dbg:
            dump(gatedA[0].v())
        if stage < 4:
            break
        gatedB = [P.alloc(f"gB{u}", NT, BF16) for u in range(16)]
        for h in range(4):
            qr = [P.alloc(f"qr{i}", NT, BF16) for i in range(2)]
            kr = [P.alloc(f"kr{i}", NT, BF16) for i in range(2)]
            vallT = P.alloc("rvall", 8 * 512, BF16)
            vall = vallT.ap; vall_lo = vallT.off; vall_hi = vallT.off + 8192

            class _VT:
                def __init__(s_, j): s_.j = j
                def v(s_, a=0, b=512):
                    return V(vall[:, s_.j * 512 + a:s_.j * 512 + b], "S", vall_lo + (s_.j * 512 + a) * 2, vall_lo + (s_.j * 512 + b) * 2)
            vtok = [_VT(j) for j in range(8)]
            t1 = P.alloc("t1", 512, BF16); t2 = P.alloc("t2", 512, BF16)
            pes = [P.alloc("pes0", 512, BF16), P.alloc("pes1", 512, BF16)]
            s0st = [P.alloc("s0st0", 512), P.alloc("s0st1", 512)]
            for which, dstl, cbase in ((0, qr, 5120), (1, kr, 6144)):
                for dt in range(2):
                    wu = wunit(wcols(wl, cbase + h * 256 + dt * 128))
                    for c in range(2):
                        sl = (c * 512, c * 512 + 512)
                        ps0 = proj_fm(wu, c)
                        pe_ = pes[(dt * 2 + c) % 2]
                        act(pe_.v(), ps0, AF.Copy)
                        ps = pe_.v()
                        R3 = "p (a b) -> p a b"
                        tt(t1.v().r(R3, b=64), ps.r(R3, b=64), ropev(dt, 0, c), ALU.mult)
                        tt(t2.v().r(R3, b=64).p(0, 64), ps.r(R3, b=64).p(64, 128), ropev(dt, 2, c).p(64, 128), ALU.mult)
                        tt(t2.v().r(R3, b=64).p(64, 128), ps.r(R3, b=64).p(0, 64), ropev(dt, 2, c).p(0, 64), ALU.mult)
                        tt(dstl[dt].v(*sl), t1.v(), t2.v(), ALU.add)
            vT = P.alloc("rvT", NT, BF16)
            for u in range(4):
                wu = wunit(wcols(wl, 7168 + h * 512 + u * 128))
                for c in range(2):
                    act(vT.v(c * 512, c * 512 + 512), proj_fm(wu, c), AF.Copy)
                tb = nbank()
                for j in range(8):
                    P.tr(bankh(tb, j * 128, j * 128 + 128), vT.v(j * 128, j * 128 + 128), ident.v())
                P.copy(V(vall.rearrange("p (j c) -> p j c", c=512)[:, :, u * 128:u * 128 + 128], "S", vall_lo, vall_hi),
                       bankh(tb, 0, 1024).r("p (j c) -> p j c", c=128))
            P.release("rvT")
            S0 = [[P.alloc(f"S0_{d}{dt}", 512, BF16) for dt in range(2)] for d in range(2)]
            for d in range(2):
                for dt in range(2):
                    s0s = s0st[(d * 2 + dt) % 2]
                    P.dma(s0s.v(), Dv(s0r_d[l, d, h, dt * 128:(dt + 1) * 128, :]))
                    P.copy(S0[d][dt].v(), s0s.v(), "act")
            wug = [wunit(wcols(wl, 9216 + h * 512 + u * 128)) for u in range(4)]
            PT = [P.alloc(f"rPT{i}", 512, BF16) for i in range(8)]
            G = P.alloc("rG", 512); qd = [[P.alloc(f"qd{d}{dt}", 512, BF16) for dt in range(2)] for d in range(2)]
            sqs = [P.alloc("r_sq", 512, BF16), P.alloc("r_sq1", 512, BF16)]; sgs = [P.alloc("r_sg", 512, BF16), P.alloc("r_sg1", 512, BF16)]
            rs = P.alloc("r_rs", 512)
            for c in range(2):
                sl = (c * 512, c * 512 + 512)
                for i in range(8):
                    b = nbank()
                    for dt in range(2):
                        mm(P.bank(b), kr[dt].v(i * 128, i * 128 + 128), qr[dt].v(*sl), start=(dt == 0), stop=(dt == 1))
                    if i // 4 != c:
                        Ls = rmask[h].v(896 - 128 * i + 512 * c, 896 - 128 * i + 512 * c + 512)
                        stt(PT[i].v(), P.bank(b), keep, Ls, ALU.mult, ALU.mult)
                    else:
                        for a2 in range(2):
                            a = 2 * c + a2
                            t0 = 256 * a
                            Ls = rmask[h].v(896 - 128 * i + t0, 896 - 128 * i + t0 + 256)
                            if a == i // 2:
                                tt(PT[i].v(a2 * 256, a2 * 256 + 256), P.bank(b, a2 * 256, a2 * 256 + 256), Ls, ALU.mult)
                            else:
                                stt(PT[i].v(a2 * 256, a2 * 256 + 256), P.bank(b, a2 * 256, a2 * 256 + 256), keep, Ls, ALU.mult, ALU.mult)
                for d in range(2):
                    if d == 0:
                        act(G.v(), V(tpos1.ap[:, c * 512:c * 512 + 512], "S", tpos1.lo, tpos1.hi), AF.Exp, scale=LGF[h])
                    else:
                        act(G.v(), V(tpos1.ap[:, c * 512:c * 512 + 512], "S", tpos1.lo, tpos1.hi), AF.Exp, scale=-LGB[h], bias=1025.0 * LGB[h])
                    for dt in range(2):
                        tt(qd[d][dt].v(), qr[dt].v(*sl), G.v(), ALU.mult)
                ssb = 4
                ob = [5, 6, 7, 3]
                for u in range(4):
                    o_ps = P.bank(ob[u])
                    for i in range(8):
                        mm(o_ps, vtok[i].v(u * 128, u * 128 + 128), PT[i].v(), start=(i == 0), stop=False)
                    for d in range(2):
                        for dt in range(2):
                            mm(o_ps, S0[d][dt].v(u * 128, u * 128 + 128), qd[d][dt].v(), start=False, stop=(d == 1 and dt == 1))
                    sq = sqs[u % 2]; sg = sgs[u % 2]
                    act(sq.v(), o_ps, AF.Square)
                    gps = proj_fm(wug[u], c, grp=(0, 1, 2))
                    mm(P.bank(ssb), onesB.v(), sq.v(), start=(u == 0), stop=(u == 3))
                    act(sg.v(), gps, AF.Silu)
                    tt(gatedB[h * 4 + u].v(*sl), o_ps, sg.v(), ALU.mult)
                rstd_from(P.bank(ssb), rs.v())
                for u in range(4):
                    tt(gatedB[h * 4 + u].v(*sl), gatedB[h * 4 + u].v(*sl), rs.v(), ALU.mult)
            for nm in [f"rPT{i}" for i in range(8)] + ["rG", "r_sq", "r_sg", "r_sq1", "r_sg1", "r_rs"] + [f"qd{d}{dt}" for d in range(2) for dt in range(2)] + \
                    [f"S0_{d}{dt}" for d in range(2) for dt in range(2)] + ["qr0", "qr1"]:
                P.release(nm)
            kd = [[P.alloc(f"kd{d}{j}", 256, BF16) for j in range(2)] for d in range(2)]
            fst = [P.alloc(f"fst{i}", 512) for i in range(4)]
            fsi = 0
            for a in range(4):
                for j2 in range(2):
                    j = 2 * a + j2
                    for dt in range(2):
                        P.tr(bankh(4, dt * 128, dt * 128 + 128), kr[dt].v(j * 128, j * 128 + 128), ident.v())
                    act(kd[0][j2].v(), bankh(4, 0, 256), IDN, scale=pv(decf, j2 * 4 + h, j2 * 4 + h + 1))
                    act(kd[1][j2].v(), bankh(4, 0, 256), IDN, scale=pv(decb, j2 * 4 + h, j2 * 4 + h + 1))
                for d in range(2):
                    for dt in range(2):
                        b = nbank()
                        for j2 in range(2):
                            mm(P.bank(b), kd[d][j2].v(dt * 128, dt * 128 + 128), vtok[2 * a + j2].v(), start=(j2 == 0), stop=(j2 == 1))
                        fb = fst[fsi % 4]; fsi += 1
                        act(fb.v(), P.bank(b), AF.Copy)
                        P.dma(Dv(str_d[l, d, a, h, dt * 128:(dt + 1) * 128, :]), fb.v(), q="sp")
            for nm in [f"kd{d}{j}" for d in range(2) for j in range(2)] + ["rvall", "t1", "t2", "kr0", "kr1", "s0st0", "s0st1", "pes0", "pes1"] + [f"fst{i}" for i in range(4)]:
                P.release(nm)
            if stage < 5 and h == 0:
                break
        if l == 0 and dbg:
            dump(gatedB[0].v())
        if stage < 6:
            break
        merged = [P.alloc(f"mg{m}", NT, BF16) for m in range(8)]
        sgt = P.alloc("sgt", 512); m1 = P.alloc("m1t", 512)
        pbl, pal = pb_d[l], pa_d[l]
        for m in range(8):
            wub = [wunit(wcols(pbl, m * 128, k0=0)), wunit(wcols(pbl, m * 128, k0=8))]
            wua = wunit(wcols(pal, m * 128))
            wgb = wunit(wcols(wl, 12288 + m * 128)); wga = wunit(wcols(wl, 11264 + m * 128))
            for c in range(2):
                sl = (c * 512, c * 512 + 512)
                b = nbank(ALL8)
                for kk in range(16):
                    mm(P.bank(b), wub[kk // 8].v((kk % 8) * 128, (kk % 8) * 128 + 128), gatedB[kk].v(*sl), start=(kk == 0), stop=(kk == 15))
                act(sgt.v(), proj_fm(wgb, c, grp=ALL8), AF.Sigmoid)
                tt(m1.v(), P.bank(b), sgt.v(), ALU.mult)
                b = nbank(ALL8)
                for kk in range(8):
                    mm(P.bank(b), wua.v(kk * 128, kk * 128 + 128), gatedA[kk].v(*sl), start=(kk == 0), stop=(kk == 7))
                act(sgt.v(), proj_fm(wga, c, grp=ALL8), AF.Sigmoid)
                tt(sgt.v(), P.bank(b), sgt.v(), ALU.mult)
                tt(merged[m].v(*sl), sgt.v(), m1.v(), ALU.add)
        wol = wout_d[l]
        for m in range(8):
            wu = wunit(wcols(wol, m * 128))
            for c in range(2):
                sl = (c * 512, c * 512 + 512)
                ps = proj_fm(wu, c, src=merged, grp=ALL8)
                stt(xT[m].v(*sl), ps, mod(l, 2, m), xT[m].v(*sl), ALU.mult, ALU.add)
        for nm in [f"mg{m}" for m in range(8)] + ["sgt", "m1t"] + [f"gA{h}" for h in range(8)] + [f"gB{u}" for u in range(16)]:
            P.release(nm)
        if l == 0 and dbg:
            dump(xT[0].v())
        if l == 0:
            do_mods(1)
        if stage < 7:
            break
        rmsnorm(1, 3, l, hT)
        actT = [P.alloc(f"ffa{j}", NT, BF16) for j in range(22)]
        ca = P.alloc("ca", NT); cg = P.alloc("cg", NT)
        wul = wup_d[l]
        for jj in range(22):
            for which, dst in ((0, ca), (1, cg)):
                ch = which * 22 + jj
                wu = wunit(wcols(wul, ch * 128))
                pb2 = ((0, 1), (2, 3), (4, 5), (6, 7))[(jj * 2 + which) % 4]
                for c in range(2):
                    for k in range(8):
                        mm(P.bank(pb2[c]), wu.v(k * 128, k * 128 + 128), hT[k].v(c * 512, c * 512 + 512), start=(k == 0), stop=(k == 7))
                w0 = pv(wconv, (l * 3 + 0) * 44 + ch, (l * 3 + 0) * 44 + ch + 1)
                w1 = pv(wconv, (l * 3 + 1) * 44 + ch, (l * 3 + 1) * 44 + ch + 1)
                w2 = pv(wconv, (l * 3 + 2) * 44 + ch, (l * 3 + 2) * 44 + ch + 1)
                w0k = pv(wck, (l * 2 + 0) * 44 + ch, (l * 2 + 0) * 44 + ch + 1)
                w2k = pv(wck, (l * 2 + 1) * 44 + ch, (l * 2 + 1) * 44 + ch + 1)
                bc = pv(bconv, l * 44 + ch, l * 44 + ch + 1)
                b0 = pb2[0]
                pp = V(P.psum[:, b0 * 512:b0 * 512 + 1024], "P", b0 * 2048, b0 * 2048 + 4096)
                R4 = "p (a b) -> p a b"
                pp4 = pp.r(R4, b=256); d4 = dst.v().r(R4, b=256)
                act(dst.v(), pp, IDN, scale=w1, bias=bc)
                stt(d4.idx(slice(None), slice(None), slice(1, 256)), pp4.idx(slice(None), slice(None), slice(0, 255)), w0,
                    d4.idx(slice(None), slice(None), slice(1, 256)), ALU.mult, ALU.add)
                stt(d4.idx(slice(None), slice(None), slice(0, 255)), pp4.idx(slice(None), slice(None), slice(1, 256)), w2,
                    d4.idx(slice(None), slice(None), slice(0, 255)), ALU.mult, ALU.add)
                stt(d4.idx(slice(None), slice(1, 4), slice(0, 1)), pp4.idx(slice(None), slice(0, 3), slice(255, 256)), w0k,
                    d4.idx(slice(None), slice(1, 4), slice(0, 1)), ALU.mult, ALU.add)
                stt(d4.idx(slice(None), slice(0, 3), slice(255, 256)), pp4.idx(slice(None), slice(1, 4), slice(0, 1)), w2k,
                    d4.idx(slice(None), slice(0, 3), slice(255, 256)), ALU.mult, ALU.add)
            act(cg.v(), cg.v(), AF.Silu)
            tt(actT[jj].v(), cg.v(), ca.v(), ALU.mult)
        P.release("ca"); P.release("cg")
        wdl = wdn_d[l]
        for m in range(8):
            wus = [wunit(wcols(wdl, m * 128, k0=0, nk=8)), wunit(wcols(wdl, m * 128, k0=8, nk=8)), wunit(wcols(wdl, m * 128, k0=16, nk=6), nk=6)]
            for c in range(2):
                sl = (c * 512, c * 512 + 512)
                b = nbank(ALL8)
                for kk in range(22):
                    mm(P.bank(b), wus[kk // 8].v((kk % 8) * 128, (kk % 8) * 128 + 128), actT[kk].v(*sl), start=(kk == 0), stop=(kk == 21))
                stt(xT[m].v(*sl), P.bank(b), mod(l, 5, m), xT[m].v(*sl), ALU.mult, ALU.add)
        for j in range(22):
            P.release(f"ffa{j}")
        if l == 0 and dbg:
            dump(xT[0].v())
    if stage >= 8:
        yo = [P.alloc(f"yo{k}", NT) for k in range(8)]
        rmsnorm(0, 0, 0, yo, final=True)
        for k in range(8):
            P.dma(Dv(yT_d[k * 128:(k + 1) * 128, :]), yo[k].v(), q="act")
    P.finish()
    return P


_CACHE = {}


def _consts(is_prompt):
    cst = np.zeros((128, 2560), np.float32)
    t = np.arange(NT)
    o = 0
    cst[:, o:o + 1024] = (t + 1)[None, :]; o += 1024
    cst[:, o:o + 1024] = (t % 32 != 0).astype(np.float32)[None, :]; o += 1024
    cst[:, o:o + 128] = np.eye(128, dtype=np.float32); o += 128
    s = np.arange(128)[:, None]; tt_ = np.arange(128)[None, :]
    same = (s // 32) == (tt_ // 32)
    cst[:, o:o + 128] = (same & (s <= tt_)).astype(np.float32); o += 128
    cst[:, o:o + 128] = (same & (s >= tt_)).astype(np.float32); o += 128
    for q in range(4):
        cst[:, o + q] = (np.arange(128) // 32 == q).astype(np.float32)
    o += 4
    kp = 0.0 if is_prompt else 1.0
    km = np.ones(64, np.float32)
    for i in (7, 15, 23):
        km[i] = kp
    for i in (8, 16, 24):
        km[32 + i] = kp
    cst[:, o:o + 64] = km[None, :]; o += 64
    cst[:, o] = kp; o += 1
    p = np.arange(128, dtype=np.float64)
    for par in range(2):
        for h in range(4):
            cst[:, o + par * 4 + h] = np.exp(LGF[h] * (255 - par * 128 - p)) / 16.0
            cst[:, o + 8 + par * 4 + h] = np.exp(LGB[h] * (par * 128 + p)) / 16.0
    o += 16
    rope = np.zeros((128, 240), np.float32)
    if is_prompt:
        rope[:, 0:16] = 1.0; rope[:, 32:96] = 1.0
    else:
        inv = 1.0 / (10000.0 ** (np.arange(64, dtype=np.float32) / 64.0))
        invp = inv[np.arange(128) % 64].astype(np.float32)
        sign = np.where(np.arange(128) < 64, -1.0, 1.0).astype(np.float32)
        r = np.arange(16, dtype=np.float32); c = np.arange(64, dtype=np.float32)
        angr = (r[None, :] * invp[:, None]).astype(np.float32)
        angc = (c[None, :] * invp[:, None]).astype(np.float32)
        rope[:, 0:16] = np.cos(angr); rope[:, 16:32] = np.sin(angr) * sign[:, None]
        rope[:, 32:96] = np.cos(angc); rope[:, 96:160] = np.sin(angc) * sign[:, None]
    rope[0:64, 160:176] = rope[64:128, 16:32]; rope[64:128, 160:176] = rope[0:64, 16:32]
    rope[0:64, 176:240] = rope[64:128, 96:160]; rope[64:128, 176:240] = rope[0:64, 96:160]
    return cst, rope


def _rmask():
    L = np.zeros((4, 128, 1920), np.float64)
    s = np.arange(128)[:, None]; c = np.arange(1920)[None, :]
    dl = c - 896 - s
    for h in range(4):
        L[h] = np.where(dl > 0, np.exp(LGF[h] * np.maximum(dl, 0)), np.where(dl < 0, np.exp(LGB[h] * np.maximum(-dl, 0)), 2.0)) / 16.0
    return L.astype(np.float32).astype(ml_dtypes.bfloat16)


def kernel(x_prompt, x_sample, state_hgrn, state_ret, c, c_ctx, norm1, norm2, final_norm, w_mod, b_mod, w_in,
           hgrn_lb_raw, p_a, p_b, w_out, w_up, w_conv, b_conv, w_down, _stage=99, _dbg=None):
    f = lambda a: np.ascontiguousarray(np.asarray(a, dtype=np.float32))
    x_prompt, x_sample, state_hgrn, state_ret = f(x_prompt), f(x_sample), f(state_hgrn), f(state_ret)
    key = (_stage, _dbg)
    if key not in _CACHE:
        nc = bass.Bass("TRN2", target_bir_lowering=False)
        build(nc, stage=_stage, dbg=_dbg)
        _CACHE[key] = nc
    nc = _CACHE[key]
    shared = {
        "n1": f(np.asarray(norm1).reshape(2, 8, 128).transpose(2, 0, 1).reshape(128, 16)),
        "n2": f(np.asarray(norm2).reshape(2, 8, 128).transpose(2, 0, 1).reshape(128, 16)),
        "nf": f(np.asarray(final_norm).reshape(8, 128).T),
        "bmod": f(np.asarray(b_mod).reshape(2, 48, 128).transpose(2, 0, 1).reshape(128, 96)),
        "w_mod": f(w_mod),
        "lbraw": f(np.asarray(hgrn_lb_raw).reshape(2, 2, 8, 128).transpose(3, 0, 1, 2).reshape(128, 32)),
        "wconv": f(np.asarray(w_conv).reshape(2, 3, 44, 128).transpose(3, 0, 1, 2).reshape(128, 264)),
        "bconv": f(np.asarray(b_conv).reshape(2, 44, 128).transpose(2, 0, 1).reshape(128, 88)),
        "w_in": f(w_in), "p_a": f(p_a), "p_b": f(p_b), "w_out": f(w_out), "w_up": f(w_up), "w_down": f(w_down),
        "rmask": _rmask(),
    }
    cst_p, rope_p = _consts(True)
    cst_s, rope_s = _consts(False)
    zh = np.zeros((2, 2, 8, 128, 128), np.float32); zr = np.zeros((2, 2, 4, 256, 512), np.float32)
    in_maps = []
    for i in range(4):
        m = dict(shared)
        m["xT"] = f(x_prompt[4 * i:4 * i + 4].reshape(1024, 1024).T)
        m["cv"] = f(np.asarray(c_ctx).reshape(8, 128).T)
        m["s0h"] = zh; m["s0r"] = zr; m["cst"] = cst_p; m["rope"] = rope_p
        in_maps.append(m)
    for b in range(4):
        m = dict(shared)
        m["xT"] = f(x_sample[b].T)
        m["cv"] = f(np.asarray(c)[b].reshape(8, 128).T)
        m["s0h"] = f(state_hgrn[b]); m["s0r"] = f(state_ret[b]); m["cst"] = cst_s; m["rope"] = rope_s
        in_maps.append(m)
    res = run_bass_kernel_spmd(nc, in_maps, core_ids=list(range(8)))
    R = res.results
    y_prompt = np.stack([np.ascontiguousarray(R[i]["yT"].T).reshape(4, 256, 1024) for i in range(4)], 0).reshape(16, 256, 1024)
    y_sample = np.stack([np.ascontiguousarray(R[4 + b]["yT"].T) for b in range(4)], 0)
    nsh = np.concatenate([R[i]["sth"].transpose(2, 0, 1, 3, 4, 5) for i in range(4)], 0)
    nsr = np.concatenate([R[i]["str"].transpose(2, 0, 1, 3, 4, 5) for i in range(4)], 0)
    out = (y_prompt.astype(np.float32), y_sample.astype(np.float32), np.ascontiguousarray(nsh, dtype=np.float32),
           np.ascontiguousarray(nsr, dtype=np.float32))
    if _dbg:
        return out, [r["dbg"] for r in R]
    return out
```

```python
import numpy as np
import concourse.bass as bass
import concourse.mybir as mybir
from concourse.bass_utils import run_bass_kernel_spmd
from contextlib import ExitStack

F32 = mybir.dt.float32
BF16 = mybir.dt.bfloat16
AF = mybir.ActivationFunctionType
ALU = mybir.AluOpType
DSZ = {F32: 4, BF16: 2}
CELL = 128
ENGS = ("pe", "act", "dve", "pool", "sp")


class V:
    __slots__ = ("ap", "space", "lo", "hi")

    def __init__(s, ap, space, lo, hi):
        s.ap, s.space, s.lo, s.hi = ap, space, lo, hi

    def r(s, pat, **kw):
        return V(s.ap.rearrange(pat, **kw), s.space, s.lo, s.hi)

    def p(s, a, b):
        return V(s.ap[a:b], s.space, s.lo, s.hi)

    def idx(s, *key):
        return V(s.ap[key], s.space, s.lo, s.hi)


class T:
    def __init__(s, base_ap, space, off, n, dt):
        s.space, s.off, s.n, s.dt = space, off, n, dt
        s.ap = base_ap

    def v(s, a=0, b=None):
        b = s.n if b is None else b
        sz = DSZ[s.dt]
        return V(s.ap[:, a:b], s.space, s.off + a * sz, s.off + b * sz)


class Prog:
    def __init__(s, nc, sb_words=53184, ndma_sem=10):
        s.nc = nc
        s.es = ExitStack()
        s.sb_words = sb_words
        s.arenaF = s.es.enter_context(nc.sbuf_tensor("arenaF", [128, sb_words], F32))
        s.arenaH = s.arenaF[:].bitcast(BF16)
        s.psum = s.es.enter_context(nc.psum_tensor("psumT", [128, 4096], F32))
        s.free = [(0, sb_words * 4)]
        s.allocs = {}
        s.ops = {e: [] for e in ENGS}
        s.cells = {}
        s.waited = {e: {} for e in ENGS}
        s.sem = {e: s.es.enter_context(nc.semaphore("sem_" + e)) for e in ENGS}
        s.dq = {}
        for q in ("sp", "act", "pool"):
            s.dq[q] = dict(sems=[s.es.enter_context(nc.semaphore(f"dma_{q}_{i}")) for i in range(ndma_sem)],
                           cnt=[0] * ndma_sem, nxt=0)
        s.out_events = []

    def alloc(s, name, n, dt=F32):
        nbytes = ((n * DSZ[dt] + CELL - 1) // CELL) * CELL
        for i, (o, sz) in enumerate(s.free):
            if sz >= nbytes:
                if sz == nbytes:
                    s.free.pop(i)
                else:
                    s.free[i] = (o + nbytes, sz - nbytes)
                s.allocs[name] = (o, nbytes)
                if dt == F32:
                    ap = s.arenaF[:, o // 4:o // 4 + n]
                else:
                    ap = s.arenaH[:, o // 2:o // 2 + n]
                return T(ap, "S", o, n, dt)
        raise RuntimeError(f"SBUF arena full allocating {name} {nbytes} free={s.free}")

    def release(s, name):
        o, nbytes = s.allocs.pop(name)
        s.free.append((o, nbytes))
        s.free.sort()
        m = []
        for o, sz in s.free:
            if m and m[-1][0] + m[-1][1] == o:
                m[-1] = (m[-1][0], m[-1][1] + sz)
            else:
                m.append((o, sz))
        s.free = m

    def bank(s, k, a=0, b=512):
        return V(s.psum[:, k * 512 + a:k * 512 + b], "P", (k * 512 + a) * 4, (k * 512 + b) * 4)

    def _cells(s, v):
        c = 2048 if v.space == "P" else CELL
        return [(v.space, i) for i in range(v.lo // c, (v.hi - 1) // c + 1)]

    def _deps(s, eng, reads, writes):
        need = {}

        def add(ev):
            if ev is None:
                return
            k = ev[0]
            if k not in need or need[k] < ev[1]:
                need[k] = ev[1]

        for v in reads:
            if v.space == "D":
                continue
            for c in s._cells(v):
                st = s.cells.get(c)
                if st:
                    add(st["w"])
                    if v.space == "P":
                        for ev in st["r"].values():
                            if ev[0] != eng:
                                add(ev)
        for v in writes:
            if v.space == "D":
                continue
            for c in s._cells(v):
                st = s.cells.get(c)
                if st:
                    add(st["w"])
                    for ev in st["r"].values():
                        add(ev)
        out = []
        wd = s.waited[eng]
        for k, val in need.items():
            if k == eng and eng in ("pe", "sp"):
                continue
            if wd.get(k, 0) >= val:
                continue
            wd[k] = val
            out.append((k, val))
        return out

    def _mark(s, ev, reads, writes):
        for v in reads:
            if v.space == "D":
                continue
            for c in s._cells(v):
                st = s.cells.setdefault(c, {"w": None, "r": {}})
                st["r"][ev[0]] = ev
        for v in writes:
            if v.space == "D":
                continue
            for c in s._cells(v):
                s.cells[c] = {"w": ev, "r": {}}

    def op(s, eng, fn, reads, writes):
        deps = s._deps(eng, reads, writes)
        seq = len(s.ops[eng]) + 1
        s.ops[eng].append(dict(fn=fn, deps=deps, dma=None))
        s._mark((eng, seq), reads, writes)

    def dma(s, out, in_, q="sp"):
        dq = s.dq[q]
        i = dq["nxt"]
        dq["nxt"] = (i + 1) % len(dq["sems"])
        key = ("dma", q, i)
        deps = s._deps(q, [in_], [out])
        prev = dq["cnt"][i]
        if prev > 0 and s.waited[q].get(key, 0) < prev:
            s.waited[q][key] = prev
            deps.append((key, prev))
        dq["cnt"][i] = prev + 16
        ev = (key, prev + 16)
        o_ap, i_ap = out.ap, in_.ap
        s.ops[q].append(dict(fn=lambda e: e.dma_start(out=o_ap, in_=i_ap), deps=deps, dma=(q, i)))
        s._mark(ev, [in_], [out])
        if out.space == "D":
            s.out_events.append(ev)

    def mm(s, out, lhsT, rhs, start=True, stop=True):
        o, l, r = out.ap, lhsT.ap, rhs.ap
        s.op("pe", lambda e: e.matmul(o, l, r, start=start, stop=stop, skip_group_check=True), [lhsT, rhs], [out])

    def tr(s, out, in_, ident):
        o, i, d = out.ap, in_.ap, ident.ap
        s.op("pe", lambda e: e.transpose(o, i, d), [in_, ident], [out])

    def act(s, out, in_, func, scale=1.0, bias=0.0, accum=None):
        o, i = out.ap, in_.ap
        sc = scale.ap if isinstance(scale, V) else scale
        bi = bias.ap if isinstance(bias, V) else bias
        rd = [in_] + [x for x in (scale, bias) if isinstance(x, V)]
        wr = [out]
        kw = {}
        if accum is not None:
            kw["accum_out"] = accum.ap
            wr.append(accum)
        s.op("act", lambda e: e.activation(o, i, func, bias=bi, scale=sc, **kw), rd, wr)

    def tt(s, out, a, b, op, eng="dve"):
        o, x, y = out.ap, a.ap, b.ap
        s.op(eng, lambda e: e.tensor_tensor(o, x, y, op), [a, b], [out])

    def ts(s, out, a, s1, s2, op0, op1=None, eng="dve"):
        o, x = out.ap, a.ap
        c1 = s1.ap if isinstance(s1, V) else s1
        c2 = s2.ap if isinstance(s2, V) else s2
        rd = [a] + [x_ for x_ in (s1, s2) if isinstance(x_, V)]
        if op1 is None:
            s.op(eng, lambda e: e.tensor_scalar(o, x, c1, None, op0), rd, [out])
        else:
            s.op(eng, lambda e: e.tensor_scalar(o, x, c1, c2, op0, op1), rd, [out])

    def stt(s, out, a, sc, b, op0, op1, eng="dve"):
        o, x, y = out.ap, a.ap, b.ap
        c = sc.ap if isinstance(sc, V) else sc
        rd = [a, b] + ([sc] if isinstance(sc, V) else [])
        s.op(eng, lambda e: e.scalar_tensor_tensor(o, x, c, y, op0, op1), rd, [out])

    def scan(s, out, d0, d1, init, op0, op1):
        o, x, y = out.ap, d0.ap, d1.ap
        s.op("dve", lambda e: e.tensor_tensor_scan(o, x, y, init, op0, op1), [d0, d1], [out])

    def copy(s, out, in_, eng="dve"):
        o, i = out.ap, in_.ap
        if eng == "act":
            s.op("act", lambda e: e.copy(o, i), [in_], [out])
        else:
            s.op(eng, lambda e: e.tensor_copy(o, i), [in_], [out])

    def memset(s, out, val, eng="dve"):
        o = out.ap
        s.op(eng, lambda e: e.memset(o, val), [], [out])

    def finish(s):
        nc = s.nc
        fin = {}
        for k, val in s.out_events:
            fin[k] = max(fin.get(k, 0), val)
        sig = {e: set() for e in ENGS}
        for e in ENGS:
            for o in s.ops[e]:
                for k, val in o["deps"]:
                    if isinstance(k, str):
                        sig[k].add(val)
        rank = {}
        for e in ENGS:
            rank[e] = {seq: i + 1 for i, seq in enumerate(sorted(sig[e]))}
        s.nsig = {e: len(rank[e]) for e in ENGS}
        engmap = {"pe": "tensor", "act": "scalar", "dve": "vector", "pool": "gpsimd", "sp": "sync"}

        def run(ename):
            def body(eng):
                for seq, o in enumerate(s.ops[ename], start=1):
                    for k, val in o["deps"]:
                        if isinstance(k, str):
                            eng.wait_ge(s.sem[k], rank[k][val])
                        else:
                            eng.wait_ge(s.dq[k[1]]["sems"][k[2]], val)
                    ins = o["fn"](eng)
                    if o["dma"] is not None:
                        q, i = o["dma"]
                        ins.then_inc(s.dq[q]["sems"][i], 16)
                    elif seq in rank[ename]:
                        ins.then_inc(s.sem[ename], 1)
                if ename == "sp":
                    for k, val in fin.items():
                        eng.wait_ge(s.dq[k[1]]["sems"][k[2]], val)
            return body

        with nc.Block() as block:
            for e in ENGS:
                getattr(block, engmap[e])(run(e))
        s.es.close()

import math
import ml_dtypes

NT = 1024
DM = 1024
IDN = AF.Identity
LGF = [math.log1p(-2.0 ** (-(5.0 + h))) for h in range(4)]
LGB = [math.log1p(-2.0 ** (-(5.5 + h))) for h in range(4)]
EPS = 1e-6


def Dv(ap):
    return V(ap, "D", 0, 0)


def build(nc, stage=99, dbg=None):
    dr = {}

    def din(name, shape, dt=F32):
        dr[name] = nc.dram_tensor(name, list(shape), dt, kind="ExternalInput").ap()
        return dr[name]

    xT_d = din("xT", [DM, NT])
    cv_d = din("cv", [128, 8])
    n1_d = din("n1", [128, 16]); n2_d = din("n2", [128, 16]); nf_d = din("nf", [128, 8])
    bmod_d = din("bmod", [128, 96])
    wmod_d = din("w_mod", [2, DM, 6144])
    lbraw_d = din("lbraw", [128, 32])
    wconv_d = din("wconv", [128, 2 * 3 * 44]); bconv_d = din("bconv", [128, 88])
    win_d = din("w_in", [2, DM, 13312]); pa_d = din("p_a", [2, 1024, 1024]); pb_d = din("p_b", [2, 2048, 1024])
    wout_d = din("w_out", [2, 1024, 1024]); wup_d = din("w_up", [2, 1024, 5632]); wdn_d = din("w_down", [2, 2816, 1024])
    s0h_d = din("s0h", [2, 2, 8, 128, 128]); s0r_d = din("s0r", [2, 2, 4, 256, 512])
    rope_d = din("rope", [128, 240])
    rmask_d = din("rmask", [4, 128, 1920], BF16)
    cst_d = din("cst", [128, 2560])
    yT_d = nc.dram_tensor("yT", [DM, NT], F32, kind="ExternalOutput").ap()
    sth_d = nc.dram_tensor("sth", [2, 2, 4, 8, 128, 128], F32, kind="ExternalOutput").ap()
    str_d = nc.dram_tensor("str", [2, 2, 4, 4, 256, 512], F32, kind="ExternalOutput").ap()
    if dbg:
        dbg_d = nc.dram_tensor("dbg", [dbg, 128, NT], F32, kind="ExternalOutput").ap()
    dbgi = [0]

    P = Prog(nc)
    mm, act, tt, ts, stt = P.mm, P.act, P.tt, P.ts, P.stt

    def dump(v, n=NT):
        if dbg and dbgi[0] < dbg:
            t = P.alloc("dbgt", n)
            P.copy(t.v(), v)
            P.dma(Dv(dbg_d[dbgi[0]][:, 0:n]), t.v(), q="act")
            P.release("dbgt")
            dbgi[0] += 1

    cst = P.alloc("cst", 2560)
    P.dma(cst.v(), Dv(cst_d))
    o = 0
    tpos1 = cst.v(o, o + 1024); o += 1024
    rst = cst.v(o, o + 1024); o += 1024
    identf = cst.v(o, o + 128); o += 128
    bmf = cst.v(o, o + 128); o += 128
    bmb = cst.v(o, o + 128); o += 128
    rowmask = cst.v(o, o + 4); o += 4
    kmask = cst.v(o, o + 64); o += 64
    keep = cst.v(o, o + 1); o += 1
    decf = cst.v(o, o + 8); o += 8
    decb = cst.v(o, o + 8); o += 8
    ident = P.alloc("ident", 128, BF16); P.copy(ident.v(), identf)
    onesD = P.alloc("onesD", 128, BF16); P.memset(onesD.v(), 1.0 / 1024)
    onesA = P.alloc("onesA", 128, BF16); P.memset(onesA.v(), 1.0 / 128)
    onesB = P.alloc("onesB", 128, BF16); P.memset(onesB.v(), 1.0 / 512)
    ropet = P.alloc("ropet", 240)
    P.dma(ropet.v(), Dv(rope_d))

    def ropev(dt, which, c):
        if dt == 0:
            o_ = (160 if which == 2 else which * 16) + 8 * c
            return V(ropet.ap[:, o_:o_ + 8].unsqueeze(2).broadcast_to([128, 8, 64]), "S", ropet.off, ropet.off + 960)
        o_ = 176 if which == 2 else 32 + which * 64
        return V(ropet.ap[:, o_:o_ + 64].unsqueeze(1).broadcast_to([128, 8, 64]), "S", ropet.off, ropet.off + 960)
    rmask = [P.alloc(f"rmask{h}", 1920, BF16) for h in range(4)]
    for h in range(4):
        P.dma(rmask[h].v(), Dv(rmask_d[h]))
    prm = P.alloc("prm", 8 + 16 + 16 + 8 + 96 + 32 + 264 + 88)
    o = 0
    cv = prm.v(o, o + 8); P.dma(cv, Dv(cv_d)); o += 8
    n1 = prm.v(o, o + 16); P.dma(n1, Dv(n1_d)); o += 16
    n2 = prm.v(o, o + 16); P.dma(n2, Dv(n2_d)); o += 16
    nf = prm.v(o, o + 8); P.dma(nf, Dv(nf_d)); o += 8
    bmod = prm.v(o, o + 96); P.dma(bmod, Dv(bmod_d)); o += 96
    lbraw = prm.v(o, o + 32); P.dma(lbraw, Dv(lbraw_d)); o += 32
    wconv = prm.v(o, o + 264); P.dma(wconv, Dv(wconv_d)); o += 264
    bconv = prm.v(o, o + 88); P.dma(bconv, Dv(bconv_d)); o += 88
    prm_off = prm.off

    def pv(base_v, a, b):
        return V(base_v.ap[:, a:b], "S", base_v.lo + a * 4, base_v.lo + b * 4)

    xT = [P.alloc(f"xT{k}", NT) for k in range(8)]
    for k in range(8):
        P.dma(xT[k].v(), Dv(xT_d[k * 128:(k + 1) * 128, :]))
    hT = [P.alloc(f"hT{k}", NT, BF16) for k in range(8)]

    sm = P.alloc("sm", 8 + 96 + 96 + 16 + 16 + 16 + 16 + 264)
    o = 0
    cs = sm.v(o, o + 8); o += 8
    mods = sm.v(o, o + 96); o += 96
    g12 = sm.v(o, o + 32); o += 32
    lb = sm.v(o, o + 32); o += 32
    oml = sm.v(o, o + 32); o += 32
    noml = sm.v(o, o + 32); o += 32
    wck = sm.v(o, o + 176); o += 176
    act(cs, cv, AF.Silu)
    P.memset(pv(lb, 0, 16), 0.0)
    tt(pv(lb, 16, 32), pv(lbraw, 16, 32), pv(lbraw, 0, 16), ALU.subtract)
    act(pv(lb, 16, 32), pv(lb, 16, 32), AF.Sigmoid)
    ts(oml, lb, -1.0, 1.0, ALU.mult, ALU.add)
    ts(noml, oml, -1.0, None, ALU.mult)
    for l in range(2):
        for j, tap in enumerate((0, 2)):
            ts(pv(wck, (l * 2 + j) * 44, (l * 2 + j + 1) * 44), pv(wconv, (l * 3 + tap) * 44, (l * 3 + tap + 1) * 44),
               keep, None, ALU.mult)

    NST, NBF, LA = 3, 10, 4
    wst = [P.alloc(f"wst{i}", 1024) for i in range(NST)]
    wbf = [P.alloc(f"wbf{i}", 1024, BF16) for i in range(NBF)]

    def wcols(w_l, c0, k0=0, nk=8):
        return w_l[k0 * 128:(k0 + nk) * 128, c0:c0 + 128].rearrange("(k p) c -> p k c", p=128)

    WQ = []
    for l_ in range(2):
        wl_ = win_d[l_]
        for h_ in range(8):
            for cb in (0, 1024, 2048, 3072, 4096):
                WQ.append((wcols(wl_, cb + h_ * 128), 8))
        for h_ in range(4):
            for cb in (5120, 6144):
                for dt_ in range(2):
                    WQ.append((wcols(wl_, cb + h_ * 256 + dt_ * 128), 8))
            for cb in (7168, 9216):
                for u_ in range(4):
                    WQ.append((wcols(wl_, cb + h_ * 512 + u_ * 128), 8))
        for m_ in range(8):
            WQ.append((wcols(pb_d[l_], m_ * 128, k0=0), 8)); WQ.append((wcols(pb_d[l_], m_ * 128, k0=8), 8))
            WQ.append((wcols(pa_d[l_], m_ * 128), 8))
            WQ.append((wcols(wl_, 12288 + m_ * 128), 8)); WQ.append((wcols(wl_, 11264 + m_ * 128), 8))
        for m_ in range(8):
            WQ.append((wcols(wout_d[l_], m_ * 128), 8))
        for jj_ in range(22):
            WQ.append((wcols(wup_d[l_], jj_ * 128), 8)); WQ.append((wcols(wup_d[l_], (22 + jj_) * 128), 8))
        for m_ in range(8):
            WQ.append((wcols(wdn_d[l_], m_ * 128, k0=0, nk=8), 8)); WQ.append((wcols(wdn_d[l_], m_ * 128, k0=8, nk=8), 8))
            WQ.append((wcols(wdn_d[l_], m_ * 128, k0=16, nk=6), 6))
    wq = dict(issued=0, used=0)
    CE = []
    for l_ in range(2):
        CE += ["pool"] * 40
        CE += [("pool", "act")[i_ % 2] for i_ in range(48)]
        CE += [("pool", "act", "act")[i_ % 3] for i_ in range(40 + 8 + 44 + 24)]
    assert len(CE) == len(WQ)

    def _issue():
        i = wq["issued"]
        dap, nk = WQ[i]
        st = wst[i % NST]; bf = wbf[i % NBF]
        P.dma(st.v(0, nk * 128).r("p (k c) -> p k c", c=128), Dv(dap))
        P.copy(bf.v(0, nk * 128), st.v(0, nk * 128), CE[i])
        wq["issued"] = i + 1

    def wunit(dap, nk=8):
        i = wq["used"]
        assert str(WQ[i][0]) == str(dap) and WQ[i][1] == nk, (i, str(WQ[i][0]), str(dap))
        while wq["issued"] < min(len(WQ), i + 1 + LA):
            _issue()
        wq["used"] = i + 1
        return wbf[i % NBF]

    bk = [0]
    ALL8 = (0, 1, 2, 3, 4, 5, 6, 7)

    def nbank(grp=(0, 1, 2, 3)):
        b = grp[bk[0] % len(grp)]; bk[0] += 1
        return b

    def bankh(k, a, b):
        return V(P.psum[:, k * 512:(k + 1) * 512].bitcast(BF16)[:, a:b], "P", k * 2048 + a * 2, k * 2048 + b * 2)

    F32R = mybir.dt.float32r
    one11 = P.alloc("one11", 1); P.memset(one11.v(), 1.0)

    def r32(v):
        return V(v.ap.bitcast(F32R), v.space, v.lo, v.hi)

    def do_mods(l, part=None):
        wms = [P.alloc("wm0", 8 * 512), P.alloc("wm1", 8 * 512)]
        mrow = P.alloc("mrow", 1536)
        mb = 4
        pcs = range(4) if part is None else (range(0, 2) if part == 0 else range(2, 4))
        for pc in pcs:
            for c3 in range(3):
                j12 = pc * 3 + c3
                wm = wms[j12 % 2]
                P.dma(wm.v().r("p (k c) -> p k c", c=512),
                      Dv(wmod_d[l][:, j12 * 512:(j12 + 1) * 512].rearrange("(k p) c -> p k c", p=128)))
                b = nbank((0, 1, 2, 3))
                for k in range(8):
                    mm(P.bank(b).p(0, 1), pv(cs, k, k + 1), wm.v(k * 512, k * 512 + 512), start=(k == 0), stop=(k == 7))
                P.copy(mrow.v(c3 * 512, c3 * 512 + 512).p(0, 1), P.bank(b).p(0, 1), "act")
            for jj in range(12):
                j = pc * 12 + jj
                mm(P.bank(mb, j, j + 1), mrow.v(jj * 128, jj * 128 + 128).p(0, 1), one11.v().p(0, 1))
        c0_, c1_ = (0, 48) if part is None else ((0, 24) if part == 0 else (24, 48))
        tt(pv(mods, l * 48 + c0_, l * 48 + c1_), P.bank(mb, c0_, c1_), pv(bmod, l * 48 + c0_, l * 48 + c1_), ALU.add)
        for i, (nrm, sc_off) in enumerate(((n1, 8), (n2, 32))):
            if part is not None and i != part:
                continue
            gv = pv(g12, (l * 2 + i) * 8, (l * 2 + i) * 8 + 8)
            ts(gv, pv(mods, l * 48 + sc_off, l * 48 + sc_off + 8), 1.0, None, ALU.add)
            tt(gv, gv, pv(nrm, l * 8, l * 8 + 8), ALU.mult)
        P.release("wm0"); P.release("wm1"); P.release("mrow")

    do_mods(0, part=0)

    def mod(l, which, k):
        return pv(mods, l * 48 + which * 8 + k, l * 48 + which * 8 + k + 1)

    def rstd_from(ssb, out_v, n=512):
        act(out_v, ssb, AF.Ln, bias=EPS)
        act(out_v, out_v, AF.Exp, scale=-0.5)

    def rmsnorm(gsel, shsel, l, dst, final=False):
        sq = P.alloc("rn_sq", 512, BF16)
        rs = P.alloc("rn_rs", 512)
        tmp = P.alloc("rn_tmp", 512)
        for c in range(2):
            b = nbank()
            for k in range(8):
                act(sq.v(), xT[k].v(c * 512, c * 512 + 512), AF.Square)
                mm(P.bank(b), onesD.v(), sq.v(), start=(k == 0), stop=(k == 7))
            rstd_from(P.bank(b), rs.v())
            for k in range(8):
                tt(tmp.v(), xT[k].v(c * 512, c * 512 + 512), rs.v(), ALU.mult)
                if final:
                    ts(dst[k].v(c * 512, c * 512 + 512), tmp.v(), pv(nf, k, k + 1), None, ALU.mult)
                else:
                    act(dst[k].v(c * 512, c * 512 + 512), tmp.v(), IDN,
                        scale=pv(g12, (l * 2 + gsel) * 8 + k, (l * 2 + gsel) * 8 + k + 1), bias=mod(l, shsel, k))
        P.release("rn_sq"); P.release("rn_rs"); P.release("rn_tmp")

    def proj_fm(wu, c, src=None, grp=(0, 1, 2, 3)):
        src = src or hT
        b = nbank(grp)
        for k in range(8):
            mm(P.bank(b), wu.v(k * 128, k * 128 + 128), src[k].v(c * 512, c * 512 + 512), start=(k == 0), stop=(k == 7))
        return P.bank(b)

    for l in range(2):
        if stage < 1:
            break
        gatedA = [P.alloc(f"gA{h}", NT, BF16) for h in range(8)]
        rmsnorm(0, 0, l, hT)
        if l == 0:
            while wq["issued"] < LA:
                _issue()
            do_mods(0, part=1)
        if l == 0 and dbg:
            dump(hT[0].v()); dump(hT[7].v())
        if stage < 2:
            break
        wl = win_d[l]
        for h in range(8):
            q32 = P.alloc("q32", NT); sgl = P.alloc("sgl", NT, BF16)
            sig = P.alloc("sig", NT); la = [P.alloc("laf", NT), P.alloc("lab", NT)]
            key = [P.alloc("keyf", NT), P.alloc("keyb", NT)]
            vtok = P.alloc("vtok", 8 * 128, BF16)
            vm = [P.alloc(f"vm{q}", 8 * 128, BF16) for q in range(4)]
            wu = wunit(wcols(wl, h * 128))
            for c in range(2):
                act(q32.v(c * 512, c * 512 + 512), proj_fm(wu, c, grp=ALL8), AF.Copy)
            sigs = [sig, P.alloc("sig2", NT)]
            for d in range(2):
                wu = wunit(wcols(wl, 1024 * (1 + d) + h * 128))
                for c in range(2):
                    sl = (c * 512, c * 512 + 512)
                    act(sigs[d].v(*sl), proj_fm(wu, c, grp=ALL8), AF.Sigmoid)
            for d in range(2):
                lbv = pv(lb, (l * 2 + d) * 8 + h, (l * 2 + d) * 8 + h + 1)
                omv = pv(oml, (l * 2 + d) * 8 + h, (l * 2 + d) * 8 + h + 1)
                nomv = pv(noml, (l * 2 + d) * 8 + h, (l * 2 + d) * 8 + h + 1)
                act(la[d].v(), sigs[d].v(), AF.Ln, scale=omv, bias=lbv)
                ts(key[d].v(), sigs[d].v(), nomv, omv, ALU.mult, ALU.add)
            P.release("sig"); P.release("sig2")
            wu = wunit(wcols(wl, 3072 + h * 128))
            vT = P.alloc("vT", NT, BF16)
            for c in range(2):
                act(vT.v(c * 512, c * 512 + 512), proj_fm(wu, c, grp=ALL8), AF.Copy)
            tbv = nbank(ALL8)
            for j in range(8):
                P.tr(bankh(tbv, j * 128, j * 128 + 128), vT.v(j * 128, j * 128 + 128), ident.v())
            P.copy(vtok.v(), bankh(tbv, 0, 1024), "dve")
            for q in range(4):
                ts(vm[q].v(), bankh(tbv, 0, 1024), pv(rowmask, q, q + 1), None, ALU.mult)
            P.release("vT")
            wu = wunit(wcols(wl, 4096 + h * 128))
            for c in range(2):
                act(sgl.v(c * 512, c * 512 + 512), proj_fm(wu, c, grp=ALL8), AF.Silu)
            if l == 0 and h == 0 and dbg:
                dump(q32.v()); dump(la[0].v()); dump(key[1].v())
            D = []
            T_ = [dict() for _ in range(2)]
            for d in range(2):
                t_ = T_[d]
                t_["bsc"] = P.alloc(f"bsc{d}", NT); t_["E1"] = P.alloc(f"E1{d}", NT); t_["E2"] = P.alloc(f"E2{d}", NT)
                t_["qt"] = P.alloc(f"qt{d}", NT, BF16); t_["kt"] = P.alloc(f"kt{d}", NT, BF16)
                t_["ktok"] = P.alloc(f"ktok{d}", 8 * 128, BF16); t_["PT"] = P.alloc(f"PT{d}", 8 * 128, BF16)
                t_["Atab"] = P.alloc(f"Atab{d}", 64)
                s0 = P.alloc(f"s0{d}", 128)
                P.dma(s0.v(), Dv(s0h_d[l, d, h]))
                t_["s0"] = s0
            P.scan(T_[0]["bsc"].v(), rst, la[0].v(), 0.0, ALU.mult, ALU.add)
            P.scan(V(T_[1]["bsc"].v().ap[:, ::-1], "S", T_[1]["bsc"].off, T_[1]["bsc"].off + 4096), rst,
                   V(la[1].v().ap[:, ::-1], "S", la[1].off, la[1].off + 4096), 0.0, ALU.mult, ALU.add)
            for d in range(2):
                act(T_[d]["E1"].v(), T_[d]["bsc"].v(), AF.Exp)
                act(T_[d]["E2"].v(), T_[d]["bsc"].v(), AF.Exp, scale=-1.0)
            for d in range(2):
                t_ = T_[d]
                tt(t_["qt"].v(), q32.v(), t_["E1"].v(), ALU.mult)
                tt(t_["kt"].v(), key[d].v(), t_["E2"].v(), ALU.mult)
                e1b = t_["E1"].v().r("p (n c) -> p n c", c=32)
                col = 31 if d == 0 else 0
                P.copy(t_["Atab"].v(0, 32), e1b.idx(slice(None), slice(None), col))
                tt(t_["Atab"].v(32, 64), t_["Atab"].v(0, 32), pv(kmask, d * 32, d * 32 + 32), ALU.mult)
            for d in range(2):
                t_ = T_[d]
                tbk = nbank(ALL8)
                for j in range(8):
                    P.tr(bankh(tbk, j * 128, j * 128 + 128), t_["kt"].v(j * 128, j * 128 + 128), ident.v())
                P.copy(t_["ktok"].v(), bankh(tbk, 0, 1024), "dve")
                bm = bmf if d == 0 else bmb
                for jg in range(2):
                    b = nbank(ALL8)
                    for jj in range(4):
                        j = jg * 4 + jj
                        mm(P.bank(b, jj * 128, jj * 128 + 128), t_["kt"].v(j * 128, j * 128 + 128), t_["qt"].v(j * 128, j * 128 + 128))
                    bmv = V(bm.ap.unsqueeze(1).broadcast_to([128, 4, 128]), "S", bm.lo, bm.hi)
                    tt(t_["PT"].v(jg * 512, jg * 512 + 512).r("p (a b) -> p a b", b=128), P.bank(b).r("p (a b) -> p a b", b=128),
                       bmv, ALU.mult)
            if l == 0 and h == 0 and dbg:
                dump(T_[0]["bsc"].v()); dump(T_[0]["E2"].v())
            for d in range(2):
                t_ = T_[d]
                for nm in (f"bsc{d}", f"E1{d}", f"E2{d}", f"kt{d}", "laf" if d == 0 else "lab", "keyf" if d == 0 else "keyb"):
                    P.release(nm)
                W = [P.alloc(f"Wst{d}", 128), P.alloc(f"Wsu{d}", 128)]; Sbf = [P.alloc(f"Sbf0{d}", 128, BF16), P.alloc(f"Sbf1{d}", 128, BF16)]
                sfin = P.alloc(f"sfin{d}", 128)
                D.append(dict(qt=t_["qt"], ktok=t_["ktok"], PT=t_["PT"], Atab=t_["Atab"], W=W, Sbf=Sbf, s0=t_["s0"], sfin=sfin, prev=None,
                              order=list(range(32)) if d == 0 else list(range(31, -1, -1))))
            P.release("q32")
            ob_ = [P.alloc("obf", NT), P.alloc("obb", NT)]
            obank = [(0, 1), (6, 7)]
            kvbank = [(2, 3), (4, 5)]
            for g in range(8):
                for d in range(2):
                    st = D[d]
                    j = st["order"][4 * g] // 4
                    for q in range(4):
                        n = st["order"][4 * g + q]
                        mm(P.bank(kvbank[d][g % 2], q * 128, q * 128 + 128), st["ktok"].v(j * 128, j * 128 + 128),
                           vm[n % 4].v(j * 128, j * 128 + 128))
                    ob = obank[d][(j // 4) % 2]
                    jj = j % 4
                    mm(P.bank(ob, jj * 128, jj * 128 + 128), vtok.v(j * 128, j * 128 + 128), st["PT"].v(j * 128, j * 128 + 128),
                       start=True, stop=False)
                for q in range(4):
                    for d in range(2):
                        st = D[d]
                        i = 4 * g + q
                        n = st["order"][i]
                        j = n // 4
                        jj = j % 4
                        ob = obank[d][(j // 4) % 2]
                        Atab = st["Atab"]; W = st["W"]; prev = st["prev"]
                        ocols = P.bank(ob, jj * 128 + (n % 4) * 32, jj * 128 + (n % 4) * 32 + 32)
                        sb = st["Sbf"][i % 2]
                        kvb = P.bank(kvbank[d][g % 2], q * 128, q * 128 + 128)
                        Wn, Wp = W[i % 2], W[(i + 1) % 2]
                        if prev is None:
                            P.copy(sb.v(), st["s0"].v(), "act")
                            tt(Wn.v(), st["s0"].v(), kvb, ALU.add)
                        else:
                            act(sb.v(), Wp.v(), IDN, scale=pv(Atab.v(), 32 + prev, 32 + prev + 1))
                            stt(Wn.v(), Wp.v(), pv(Atab.v(), 32 + prev, 32 + prev + 1), kvb, ALU.mult, ALU.add)
                        mm(ocols, sb.v(), st["qt"].v(n * 32, n * 32 + 32), start=False, stop=(q == 3))
                        st["prev"] = n
                        if i % 8 == 7:
                            seq = n // 8
                            act(st["sfin"].v(), Wn.v(), IDN, scale=pv(Atab.v(), n, n + 1))
                            P.dma(Dv(sth_d[l, d, seq, h]), st["sfin"].v(), q="sp")
                        if i % 16 == 15:
                            c0 = (j // 4) * 512
                            P.copy(ob_[d].v(c0, c0 + 512), P.bank(ob), "dve")
            obuf = ob_[0]
            tt(obuf.v(), ob_[0].v(), ob_[1].v(), ALU.add)
            for d in range(2):
                for nm in (f"qt{d}", f"ktok{d}", f"PT{d}", f"Atab{d}", f"Wst{d}", f"Wsu{d}", f"Sbf0{d}", f"Sbf1{d}", f"s0{d}", f"sfin{d}"):
                    P.release(nm)
            if l == 0 and h == 0 and dbg:
                dump(obuf.v())
            sq = P.alloc("hn_sq", NT, BF16); rs = P.alloc("hn_rs", NT); tmp = P.alloc("hn_tmp", NT)
            act(sq.v(), obuf.v(), AF.Square)
            hb = [nbank(ALL8), nbank(ALL8)]
            for c in range(2):
                mm(P.bank(hb[c]), onesA.v(), sq.v(c * 512, c * 512 + 512))
            for c in range(2):
                act(rs.v(c * 512, c * 512 + 512), P.bank(hb[c]), AF.Ln, bias=EPS)
            act(rs.v(), rs.v(), AF.Exp, scale=-0.5)
            tt(tmp.v(), obuf.v(), rs.v(), ALU.mult)
            tt(gatedA[h].v(), tmp.v(), sgl.v(), ALU.mult)
            for nm in ("hn_sq", "hn_rs", "hn_tmp", "sgl", "vtok", "obf", "obb", "vm0", "vm1", "vm2", "vm3"):
                P.release(nm)
            if stage < 3 and h == 0:
                break
        if l == 0 and dbg:
            dump(gatedA[0].v())
        if stage < 4:
            break
        gatedB = [P.alloc(f"gB{u}", NT, BF16) for u in range(16)]
        for h in range(4):
            qr = [P.alloc(f"qr{i}", NT, BF16) for i in range(2)]
            kr = [P.alloc(f"kr{i}", NT, BF16) for i in range(2)]
            vallT = P.alloc("rvall", 8 * 512, BF16)
            vall = vallT.ap; vall_lo = vallT.off; vall_hi = vallT.off + 8192

            class _VT:
                def __init__(s_, j): s_.j = j
                def v(s_, a=0, b=512):
                    return V(vall[:, s_.j * 512 + a:s_.j * 512 + b], "S", vall_lo + (s_.j * 512 + a) * 2, vall_lo + (s_.j * 512 + b) * 2)
            vtok = [_VT(j) for j in range(8)]
            t1 = P.alloc("t1", 512, BF16); t2 = P.alloc("t2", 512, BF16)
            pes = [P.alloc("pes0", 512, BF16), P.alloc("pes1", 512, BF16)]
            s0st = [P.alloc("s0st0", 512), P.alloc("s0st1", 512)]
            for which, dstl, cbase in ((0, qr, 5120), (1, kr, 6144)):
                for dt in range(2):
                    wu = wunit(wcols(wl, cbase + h * 256 + dt * 128))
                    for c in range(2):
                        sl = (c * 512, c * 512 + 512)
                        ps0 = proj_fm(wu, c, grp=ALL8)
                        pe_ = pes[(dt * 2 + c) % 2]
                        act(pe_.v(), ps0, AF.Copy)
                        ps = pe_.v()
                        R3 = "p (a b) -> p a b"
                        tt(t1.v().r(R3, b=64), ps.r(R3, b=64), ropev(dt, 0, c), ALU.mult)
                        tt(t2.v().r(R3, b=64).p(0, 64), ps.r(R3, b=64).p(64, 128), ropev(dt, 2, c).p(64, 128), ALU.mult)
                        tt(t2.v().r(R3, b=64).p(64, 128), ps.r(R3, b=64).p(0, 64), ropev(dt, 2, c).p(0, 64), ALU.mult)
                        tt(dstl[dt].v(*sl), t1.v(), t2.v(), ALU.add)
            vT = P.alloc("rvT", NT, BF16)
            for u in range(4):
                wu = wunit(wcols(wl, 7168 + h * 512 + u * 128))
                for c in range(2):
                    act(vT.v(c * 512, c * 512 + 512), proj_fm(wu, c, grp=ALL8), AF.Copy)
                tb = nbank(ALL8)
                for j in range(8):
                    P.tr(bankh(tb, j * 128, j * 128 + 128), vT.v(j * 128, j * 128 + 128), ident.v())
                P.copy(V(vall.rearrange("p (j c) -> p j c", c=512)[:, :, u * 128:u * 128 + 128], "S", vall_lo, vall_hi),
                       bankh(tb, 0, 1024).r("p (j c) -> p j c", c=128))
            P.release("rvT")
            S0 = [[P.alloc(f"S0_{d}{dt}", 512, BF16) for dt in range(2)] for d in range(2)]
            for d in range(2):
                for dt in range(2):
                    s0s = s0st[(d * 2 + dt) % 2]
                    P.dma(s0s.v(), Dv(s0r_d[l, d, h, dt * 128:(dt + 1) * 128, :]))
                    P.copy(S0[d][dt].v(), s0s.v(), "act")
            wug = [wunit(wcols(wl, 9216 + h * 512 + u * 128)) for u in range(4)]
            PT = [P.alloc(f"rPT{i}", 512, BF16) for i in range(8)]
            G = P.alloc("rG", 512); qd = [[P.alloc(f"qd{d}{dt}", 512, BF16) for dt in range(2)] for d in range(2)]
            sqs = [P.alloc("r_sq", 512, BF16), P.alloc("r_sq1", 512, BF16)]; sgs = [P.alloc("r_sg", 512, BF16), P.alloc("r_sg1", 512, BF16)]
            rs = P.alloc("r_rs", 512)
            for c in range(2):
                sl = (c * 512, c * 512 + 512)
                for i in range(8):
                    b = nbank()
                    for dt in range(2):
                        mm(P.bank(b), kr[dt].v(i * 128, i * 128 + 128), qr[dt].v(*sl), start=(dt == 0), stop=(dt == 1))
                    if i // 4 != c:
                        Ls = rmask[h].v(896 - 128 * i + 512 * c, 896 - 128 * i + 512 * c + 512)
                        stt(PT[i].v(), P.bank(b), keep, Ls, ALU.mult, ALU.mult)
                    else:
                        for a2 in range(2):
                            a = 2 * c + a2
                            t0 = 256 * a
                            Ls = rmask[h].v(896 - 128 * i + t0, 896 - 128 * i + t0 + 256)
                            if a == i // 2:
                                tt(PT[i].v(a2 * 256, a2 * 256 + 256), P.bank(b, a2 * 256, a2 * 256 + 256), Ls, ALU.mult)
                            else:
                                stt(PT[i].v(a2 * 256, a2 * 256 + 256), P.bank(b, a2 * 256, a2 * 256 + 256), keep, Ls, ALU.mult, ALU.mult)
                for d in range(2):
                    if d == 0:
                        act(G.v(), V(tpos1.ap[:, c * 512:c * 512 + 512], "S", tpos1.lo, tpos1.hi), AF.Exp, scale=LGF[h])
                    else:
                        act(G.v(), V(tpos1.ap[:, c * 512:c * 512 + 512], "S", tpos1.lo, tpos1.hi), AF.Exp, scale=-LGB[h], bias=1025.0 * LGB[h])
                    for dt in range(2):
                        tt(qd[d][dt].v(), qr[dt].v(*sl), G.v(), ALU.mult)
                ssb = 4
                ob = [5, 6, 7, 3]
                for u in range(4):
                    o_ps = P.bank(ob[u])
                    for i in range(8):
                        mm(o_ps, vtok[i].v(u * 128, u * 128 + 128), PT[i].v(), start=(i == 0), stop=False)
                    for d in range(2):
                        for dt in range(2):
                            mm(o_ps, S0[d][dt].v(u * 128, u * 128 + 128), qd[d][dt].v(), start=False, stop=(d == 1 and dt == 1))
                    sq = sqs[u % 2]; sg = sgs[u % 2]
                    act(sq.v(), o_ps, AF.Square)
                    gps = proj_fm(wug[u], c, grp=(0, 1, 2))
                    mm(P.bank(ssb), onesB.v(), sq.v(), start=(u == 0), stop=(u == 3))
                    act(sg.v(), gps, AF.Silu)
                    tt(gatedB[h * 4 + u].v(*sl), o_ps, sg.v(), ALU.mult)
                rstd_from(P.bank(ssb), rs.v())
                for u in range(4):
                    tt(gatedB[h * 4 + u].v(*sl), gatedB[h * 4 + u].v(*sl), rs.v(), ALU.mult)
            for nm in [f"rPT{i}" for i in range(8)] + ["rG", "r_sq", "r_sg", "r_sq1", "r_sg1", "r_rs"] + [f"qd{d}{dt}" for d in range(2) for dt in range(2)] + \
                    [f"S0_{d}{dt}" for d in range(2) for dt in range(2)] + ["qr0", "qr1"]:
                P.release(nm)
            kd = [[P.alloc(f"kd{d}{j}", 256, BF16) for j in range(2)] for d in range(2)]
            fst = [P.alloc(f"fst{i}", 512) for i in range(4)]
            fsi = 0
            for a in range(4):
                for j2 in range(2):
                    j = 2 * a + j2
                    for dt in range(2):
                        P.tr(bankh(4, dt * 128, dt * 128 + 128), kr[dt].v(j * 128, j * 128 + 128), ident.v())
                    act(kd[0][j2].v(), bankh(4, 0, 256), IDN, scale=pv(decf, j2 * 4 + h, j2 * 4 + h + 1))
                    act(kd[1][j2].v(), bankh(4, 0, 256), IDN, scale=pv(decb, j2 * 4 + h, j2 * 4 + h + 1))
                for d in range(2):
                    for dt in range(2):
                        b = nbank()
                        for j2 in range(2):
                            mm(P.bank(b), kd[d][j2].v(dt * 128, dt * 128 + 128), vtok[2 * a + j2].v(), start=(j2 == 0), stop=(j2 == 1))
                        fb = fst[fsi % 4]; fsi += 1
                        act(fb.v(), P.bank(b), AF.Copy)
                        P.dma(Dv(str_d[l, d, a, h, dt * 128:(dt + 1) * 128, :]), fb.v(), q="sp")
            for nm in [f"kd{d}{j}" for d in range(2) for j in range(2)] + ["rvall", "t1", "t2", "kr0", "kr1", "s0st0", "s0st1", "pes0", "pes1"] + [f"fst{i}" for i in range(4)]:
                P.release(nm)
            if stage < 5 and h == 0:
                break
        if l == 0 and dbg:
            dump(gatedB[0].v())
        if stage < 6:
            break
        merged = [P.alloc(f"mg{m}", NT, BF16) for m in range(8)]
        sgt = P.alloc("sgt", 512); m1 = P.alloc("m1t", 512)
        pbl, pal = pb_d[l], pa_d[l]
        for m in range(8):
            wub = [wunit(wcols(pbl, m * 128, k0=0)), wunit(wcols(pbl, m * 128, k0=8))]
            wua = wunit(wcols(pal, m * 128))
            wgb = wunit(wcols(wl, 12288 + m * 128)); wga = wunit(wcols(wl, 11264 + m * 128))
            for c in range(2):
                sl = (c * 512, c * 512 + 512)
                b = nbank(ALL8)
                for kk in range(16):
                    mm(P.bank(b), wub[kk // 8].v((kk % 8) * 128, (kk % 8) * 128 + 128), gatedB[kk].v(*sl), start=(kk == 0), stop=(kk == 15))
                act(sgt.v(), proj_fm(wgb, c, grp=ALL8), AF.Sigmoid)
                tt(m1.v(), P.bank(b), sgt.v(), ALU.mult)
                b = nbank(ALL8)
                for kk in range(8):
                    mm(P.bank(b), wua.v(kk * 128, kk * 128 + 128), gatedA[kk].v(*sl), start=(kk == 0), stop=(kk == 7))
                act(sgt.v(), proj_fm(wga, c, grp=ALL8), AF.Sigmoid)
                tt(sgt.v(), P.bank(b), sgt.v(), ALU.mult)
                tt(merged[m].v(*sl), sgt.v(), m1.v(), ALU.add)
        wol = wout_d[l]
        for m in range(8):
            wu = wunit(wcols(wol, m * 128))
            for c in range(2):
                sl = (c * 512, c * 512 + 512)
                ps = proj_fm(wu, c, src=merged, grp=ALL8)
                stt(xT[m].v(*sl), ps, mod(l, 2, m), xT[m].v(*sl), ALU.mult, ALU.add)
        for nm in [f"mg{m}" for m in range(8)] + ["sgt", "m1t"] + [f"gA{h}" for h in range(8)] + [f"gB{u}" for u in range(16)]:
            P.release(nm)
        if l == 0 and dbg:
            dump(xT[0].v())
        if l == 0:
            do_mods(1)
        if stage < 7:
            break
        rmsnorm(1, 3, l, hT)
        actT = [P.alloc(f"ffa{j}", NT, BF16) for j in range(22)]
        ca = P.alloc("ca", NT); cg = P.alloc("cg", NT)
        wul = wup_d[l]
        for jj in range(22):
            for which, dst in ((0, ca), (1, cg)):
                ch = which * 22 + jj
                wu = wunit(wcols(wul, ch * 128))
                pb2 = ((0, 1), (2, 3), (4, 5), (6, 7))[(jj * 2 + which) % 4]
                for c in range(2):
                    for k in range(8):
                        mm(P.bank(pb2[c]), wu.v(k * 128, k * 128 + 128), hT[k].v(c * 512, c * 512 + 512), start=(k == 0), stop=(k == 7))
                w0 = pv(wconv, (l * 3 + 0) * 44 + ch, (l * 3 + 0) * 44 + ch + 1)
                w1 = pv(wconv, (l * 3 + 1) * 44 + ch, (l * 3 + 1) * 44 + ch + 1)
                w2 = pv(wconv, (l * 3 + 2) * 44 + ch, (l * 3 + 2) * 44 + ch + 1)
                w0k = pv(wck, (l * 2 + 0) * 44 + ch, (l * 2 + 0) * 44 + ch + 1)
                w2k = pv(wck, (l * 2 + 1) * 44 + ch, (l * 2 + 1) * 44 + ch + 1)
                bc = pv(bconv, l * 44 + ch, l * 44 + ch + 1)
                b0 = pb2[0]
                pp = V(P.psum[:, b0 * 512:b0 * 512 + 1024], "P", b0 * 2048, b0 * 2048 + 4096)
                R4 = "p (a b) -> p a b"
                pp4 = pp.r(R4, b=256); d4 = dst.v().r(R4, b=256)
                act(dst.v(), pp, IDN, scale=w1, bias=bc)
                stt(d4.idx(slice(None), slice(None), slice(1, 256)), pp4.idx(slice(None), slice(None), slice(0, 255)), w0,
                    d4.idx(slice(None), slice(None), slice(1, 256)), ALU.mult, ALU.add)
                stt(d4.idx(slice(None), slice(None), slice(0, 255)), pp4.idx(slice(None), slice(None), slice(1, 256)), w2,
                    d4.idx(slice(None), slice(None), slice(0, 255)), ALU.mult, ALU.add)
                stt(d4.idx(slice(None), slice(1, 4), slice(0, 1)), pp4.idx(slice(None), slice(0, 3), slice(255, 256)), w0k,
                    d4.idx(slice(None), slice(1, 4), slice(0, 1)), ALU.mult, ALU.add)
                stt(d4.idx(slice(None), slice(0, 3), slice(255, 256)), pp4.idx(slice(None), slice(1, 4), slice(0, 1)), w2k,
                    d4.idx(slice(None), slice(0, 3), slice(255, 256)), ALU.mult, ALU.add)
            act(cg.v(), cg.v(), AF.Silu)
            tt(actT[jj].v(), cg.v(), ca.v(), ALU.mult)
        P.release("ca"); P.release("cg")
        wdl = wdn_d[l]
        for m in range(8):
            wus = [wunit(wcols(wdl, m * 128, k0=0, nk=8)), wunit(wcols(wdl, m * 128, k0=8, nk=8)), wunit(wcols(wdl, m * 128, k0=16, nk=6), nk=6)]
            for c in range(2):
                sl = (c * 512, c * 512 + 512)
                b = nbank(ALL8)
                for kk in range(22):
                    mm(P.bank(b), wus[kk // 8].v((kk % 8) * 128, (kk % 8) * 128 + 128), actT[kk].v(*sl), start=(kk == 0), stop=(kk == 21))
                stt(xT[m].v(*sl), P.bank(b), mod(l, 5, m), xT[m].v(*sl), ALU.mult, ALU.add)
        for j in range(22):
            P.release(f"ffa{j}")
        if l == 0 and dbg:
            dump(xT[0].v())
    if stage >= 8:
        yo = [P.alloc(f"yo{k}", NT) for k in range(8)]
        rmsnorm(0, 0, 0, yo, final=True)
        for k in range(8):
            P.dma(Dv(yT_d[k * 128:(k + 1) * 128, :]), yo[k].v(), q="act")
    P.finish()
    return P


_CACHE = {}


def _consts(is_prompt):
    cst = np.zeros((128, 2560), np.float32)
    t = np.arange(NT)
    o = 0
    cst[:, o:o + 1024] = (t + 1)[None, :]; o += 1024
    cst[:, o:o + 1024] = (t % 32 != 0).astype(np.float32)[None, :]; o += 1024
    cst[:, o:o + 128] = np.eye(128, dtype=np.float32); o += 128
    s = np.arange(128)[:, None]; tt_ = np.arange(128)[None, :]
    same = (s // 32) == (tt_ // 32)
    cst[:, o:o + 128] = (same & (s <= tt_)).astype(np.float32); o += 128
    cst[:, o:o + 128] = (same & (s >= tt_)).astype(np.float32); o += 128
    for q in range(4):
        cst[:, o + q] = (np.arange(128) // 32 == q).astype(np.float32)
    o += 4
    kp = 0.0 if is_prompt else 1.0
    km = np.ones(64, np.float32)
    for i in (7, 15, 23):
        km[i] = kp
    for i in (8, 16, 24):
        km[32 + i] = kp
    cst[:, o:o + 64] = km[None, :]; o += 64
    cst[:, o] = kp; o += 1
    p = np.arange(128, dtype=np.float64)
    for par in range(2):
        for h in range(4):
            cst[:, o + par * 4 + h] = np.exp(LGF[h] * (255 - par * 128 - p)) / 16.0
            cst[:, o + 8 + par * 4 + h] = np.exp(LGB[h] * (par * 128 + p)) / 16.0
    o += 16
    rope = np.zeros((128, 240), np.float32)
    if is_prompt:
        rope[:, 0:16] = 1.0; rope[:, 32:96] = 1.0
    else:
        inv = 1.0 / (10000.0 ** (np.arange(64, dtype=np.float32) / 64.0))
        invp = inv[np.arange(128) % 64].astype(np.float32)
        sign = np.where(np.arange(128) < 64, -1.0, 1.0).astype(np.float32)
        r = np.arange(16, dtype=np.float32); c = np.arange(64, dtype=np.float32)
        angr = (r[None, :] * invp[:, None]).astype(np.float32)
        angc = (c[None, :] * invp[:, None]).astype(np.float32)
        rope[:, 0:16] = np.cos(angr); rope[:, 16:32] = np.sin(angr) * sign[:, None]
        rope[:, 32:96] = np.cos(angc); rope[:, 96:160] = np.sin(angc) * sign[:, None]
    rope[0:64, 160:176] = rope[64:128, 16:32]; rope[64:128, 160:176] = rope[0:64, 16:32]
    rope[0:64, 176:240] = rope[64:128, 96:160]; rope[64:128, 176:240] = rope[0:64, 96:160]
    return cst, rope


def _rmask():
    L = np.zeros((4, 128, 1920), np.float64)
    s = np.arange(128)[:, None]; c = np.arange(1920)[None, :]
    dl = c - 896 - s
    for h in range(4):
        L[h] = np.where(dl > 0, np.exp(LGF[h] * np.maximum(dl, 0)), np.where(dl < 0, np.exp(LGB[h] * np.maximum(-dl, 0)), 2.0)) / 16.0
    return L.astype(np.float32).astype(ml_dtypes.bfloat16)


def kernel(x_prompt, x_sample, state_hgrn, state_ret, c, c_ctx, norm1, norm2, final_norm, w_mod, b_mod, w_in,
           hgrn_lb_raw, p_a, p_b, w_out, w_up, w_conv, b_conv, w_down, _stage=99, _dbg=None):
    f = lambda a: np.ascontiguousarray(np.asarray(a, dtype=np.float32))
    x_prompt, x_sample, state_hgrn, state_ret = f(x_prompt), f(x_sample), f(state_hgrn), f(state_ret)
    key = (_stage, _dbg)
    if key not in _CACHE:
        nc = bass.Bass("TRN2", target_bir_lowering=False)
        build(nc, stage=_stage, dbg=_dbg)
        _CACHE[key] = nc
    nc = _CACHE[key]
    shared = {
        "n1": f(np.asarray(norm1).reshape(2, 8, 128).transpose(2, 0, 1).reshape(128, 16)),
        "n2": f(np.asarray(norm2).reshape(2, 8, 128).transpose(2, 0, 1).reshape(128, 16)),
        "nf": f(np.asarray(final_norm).reshape(8, 128).T),
        "bmod": f(np.asarray(b_mod).reshape(2, 48, 128).transpose(2, 0, 1).reshape(128, 96)),
        "w_mod": f(w_mod),
        "lbraw": f(np.asarray(hgrn_lb_raw).reshape(2, 2, 8, 128).transpose(3, 0, 1, 2).reshape(128, 32)),
        "wconv": f(np.asarray(w_conv).reshape(2, 3, 44, 128).transpose(3, 0, 1, 2).reshape(128, 264)),
        "bconv": f(np.asarray(b_conv).reshape(2, 44, 128).transpose(2, 0, 1).reshape(128, 88)),
        "w_in": f(w_in), "p_a": f(p_a), "p_b": f(p_b), "w_out": f(w_out), "w_up": f(w_up), "w_down": f(w_down),
        "rmask": _rmask(),
    }
    cst_p, rope_p = _consts(True)
    cst_s, rope_s = _consts(False)
    zh = np.zeros((2, 2, 8, 128, 128), np.float32); zr = np.zeros((2, 2, 4, 256, 512), np.float32)
    in_maps = []
    for i in range(4):
        m = dict(shared)
        m["xT"] = f(x_prompt[4 * i:4 * i + 4].reshape(1024, 1024).T)
        m["cv"] = f(np.asarray(c_ctx).reshape(8, 128).T)
        m["s0h"] = zh; m["s0r"] = zr; m["cst"] = cst_p; m["rope"] = rope_p
        in_maps.append(m)
    for b in range(4):
        m = dict(shared)
        m["xT"] = f(x_sample[b].T)
        m["cv"] = f(np.asarray(c)[b].reshape(8, 128).T)
        m["s0h"] = f(state_hgrn[b]); m["s0r"] = f(state_ret[b]); m["cst"] = cst_s; m["rope"] = rope_s
        in_maps.append(m)
    res = run_bass_kernel_spmd(nc, in_maps, core_ids=list(range(8)))
    R = res.results
    y_prompt = np.stack([np.ascontiguousarray(R[i]["yT"].T).reshape(4, 256, 1024) for i in range(4)], 0).reshape(16, 256, 1024)
    y_sample = np.stack([np.ascontiguousarray(R[4 + b]["yT"].T) for b in range(4)], 0)
    nsh = np.concatenate([R[i]["sth"].transpose(2, 0, 1, 3, 4, 5) for i in range(4)], 0)
    nsr = np.concatenate([R[i]["str"].transpose(2, 0, 1, 3, 4, 5) for i in range(4)], 0)
    out = (y_prompt.astype(np.float32), y_sample.astype(np.float32), np.ascontiguousarray(nsh, dtype=np.float32),
           np.ascontiguousarray(nsr, dtype=np.float32))
    if _dbg:
        return out, [r["dbg"] for r in R]
    return out
```
